# Optimizing a Trainium2 kernel written in Bass

```python
import math
import jax, jax.numpy as jnp
from jax import lax
import numpy as np

D_MODEL = 1024
BATCH = 8
SEQ = 2048
DEPTH = 1
DEC_BATCH = 128
DEC_SEQ = 4
PAST_LEN = 16384
PAGE_SIZE = 128

MIX_SSM = D_MODEL // 2
MIX_CONV = D_MODEL - MIX_SSM
SSM_GROUP_CH = 16
SSM_GROUPS = MIX_SSM // SSM_GROUP_CH
SSM_STATE = 64
CONV_HEADS = 8
CONV_K = 3
N_MEM = 256
XATTN_HEADS = 4
XATTN_HEAD_DIM = D_MODEL // XATTN_HEADS
D_FF = 256 * ((8 * D_MODEL // 3 + 255) // 256)
FFN_CONV_K = 3
EPS = 1e-6

kernel_name = "hymba_s5_shortconv_memxattn_step"

F32 = jnp.float32


def rms_norm(x, g):
    xf = x.astype(F32)
    y = xf * lax.rsqrt(jnp.mean(xf * xf, axis=-1, keepdims=True) + EPS)
    return (y * g.astype(F32)).astype(x.dtype)


def causal_dwconv(x, w, prev):
    k = w.shape[0]
    t = x.shape[1]
    xp = jnp.concatenate([prev.astype(x.dtype), x], axis=1)
    y = xp[:, 0:t] * w[0]
    for j in range(1, k):
        y = y + xp[:, j:j + t] * w[j]
    return y, xp[:, t:]


def _complex_affine_combine(e1, e2):
    ar1, ai1, br1, bi1 = e1
    ar2, ai2, br2, bi2 = e2
    return (ar2 * ar1 - ai2 * ai1,
            ar2 * ai1 + ai2 * ar1,
            ar2 * br1 - ai2 * bi1 + br2,
            ar2 * bi1 + ai2 * br1 + bi2)


def s5_ssm(u, A_re, A_im, log_dt, B_re, B_im, C_re, C_im, D, s0):
    b, t, _ = u.shape
    G, P, N = SSM_GROUPS, SSM_GROUP_CH, SSM_STATE
    uf = u.astype(F32).reshape(b, t, G, P)
    dt = jnp.exp(log_dt.astype(F32))[:, None]
    lr = A_re.astype(F32)
    li = A_im.astype(F32)
    mag = jnp.exp(dt * lr)
    ph = dt * li
    ar = mag * jnp.cos(ph)
    ai = mag * jnp.sin(ph)
    den = lr * lr + li * li
    cr = ((ar - 1.0) * lr + ai * li) / den
    ci = (ai * lr - (ar - 1.0) * li) / den
    Br = B_re.astype(F32)
    Bi = B_im.astype(F32)
    bbr = cr[..., None] * Br - ci[..., None] * Bi
    bbi = cr[..., None] * Bi + ci[..., None] * Br
    bu_r = jnp.einsum('gnp,btgp->tbgn', bbr, uf)
    bu_i = jnp.einsum('gnp,btgp->tbgn', bbi, uf)
    a_r = jnp.broadcast_to(ar[None, None], (t, 1, G, N))
    a_i = jnp.broadcast_to(ai[None, None], (t, 1, G, N))
    acum_r, acum_i, s_r, s_i = lax.associative_scan(
        _complex_affine_combine, (a_r, a_i, bu_r, bu_i), axis=0)
    if s0 is not None:
        s0r = s0[0].astype(F32)[None]
        s0i = s0[1].astype(F32)[None]
        s_r, s_i = (s_r + acum_r * s0r - acum_i * s0i,
                    s_i + acum_r * s0i + acum_i * s0r)
    y = (jnp.einsum('gpn,tbgn->btgp', C_re.astype(F32), s_r)
         - jnp.einsum('gpn,tbgn->btgp', C_im.astype(F32), s_i)
         + D.astype(F32) * uf)
    return (y.reshape(b, t, MIX_SSM).astype(u.dtype),
            s_r[-1].astype(u.dtype), s_i[-1].astype(u.dtype))


def memory_kv(mem, g, w_k, w_v):
    b = mem.shape[0]
    m = rms_norm(mem, g)
    k = (m @ w_k).reshape(b, N_MEM, XATTN_HEADS, XATTN_HEAD_DIM)
    v = (m @ w_v).reshape(b, N_MEM, XATTN_HEADS, XATTN_HEAD_DIM)
    return k, v


def cross_attention(h, k, v, w_q, w_xo):
    b, t, _ = h.shape
    q = (h @ w_q).reshape(b, t, XATTN_HEADS, XATTN_HEAD_DIM)
    s = jnp.einsum('bthd,bmhd->bhtm', q, k.astype(q.dtype)).astype(F32) * (XATTN_HEAD_DIM ** -0.5)
    pr = jax.nn.softmax(s, axis=-1).astype(h.dtype)
    o = jnp.einsum('bhtm,bmhd->bthd', pr, v.astype(h.dtype)).reshape(b, t, XATTN_HEADS * XATTN_HEAD_DIM)
    return o @ w_xo


def decoder_layer(x, mem_k, mem_v, prev, p):
    b = x.shape[0]
    if prev is None:
        s0 = None
        conv_prev = jnp.zeros((b, CONV_K - 1, MIX_CONV), x.dtype)
        ffn_prev = jnp.zeros((b, FFN_CONV_K - 1, D_FF), x.dtype)
    else:
        s0 = (prev[0], prev[1])
        conv_prev, ffn_prev = prev[2], prev[3]

    h = rms_norm(x, p['norm_mix'])
    z = h @ p['w_in']
    u, xin, bg, cg = jnp.split(z, [MIX_SSM, MIX_SSM + MIX_CONV, MIX_SSM + 2 * MIX_CONV], axis=-1)
    y_ssm, s_re, s_im = s5_ssm(u, p['ssm_A_re'], p['ssm_A_im'], p['ssm_log_dt'], p['ssm_B_re'],
                               p['ssm_B_im'], p['ssm_C_re'], p['ssm_C_im'], p['ssm_D'], s0)
    y_ssm = jax.nn.gelu(y_ssm)
    y_ssm = y_ssm * jax.nn.sigmoid(y_ssm @ p['w_glu'] + p['b_glu'])
    conv_y, conv_buf = causal_dwconv(cg * xin, p['conv_w'], conv_prev)
    y_conv = bg * conv_y
    mixed = jnp.concatenate([rms_norm(y_ssm, p['norm_ssm_out']),
                             rms_norm(y_conv, p['norm_conv_out'])], axis=-1)
    x = x + mixed @ p['w_out']

    x = x + cross_attention(rms_norm(x, p['norm_xattn']), mem_k, mem_v, p['w_q'], p['w_xo'])

    h = rms_norm(x, p['norm_ffn'])
    a, ffn_buf = causal_dwconv(h @ p['w_up'], p['ffn_conv_w'], ffn_prev)
    x = x + (jax.nn.gelu(a) * (h @ p['w_gate'])) @ p['w_down']
    return x, (s_re, s_im, conv_buf, ffn_buf)


def setup_inputs(seed: int = 0) -> dict:
    key = jax.random.key(seed)
    ks = iter(jax.random.split(key, 48))
    L, D = DEPTH, D_MODEL
    G, N, P = SSM_GROUPS, SSM_STATE, SSM_GROUP_CH

    def nrm(shape, scale):
        return scale * jax.random.normal(next(ks), shape, F32)

    def gain(shape):
        return 1.0 + nrm(shape, 0.02)

    n_idx = jnp.arange(N, dtype=F32)
    return {
        "x_prompt": nrm((BATCH, SEQ, D), 1.0),
        "x_sample": nrm((DEC_BATCH, DEC_SEQ, D), 1.0),
        "mem_prompt": nrm((BATCH, N_MEM, D), 1.0),
        "cache_mem_k": nrm((L, DEC_BATCH, N_MEM, XATTN_HEADS, XATTN_HEAD_DIM), 1.0),
        "cache_mem_v": nrm((L, DEC_BATCH, N_MEM, XATTN_HEADS, XATTN_HEAD_DIM), 1.0),
        "state_ssm_re": nrm((L, DEC_BATCH, G, N), 0.1),
        "state_ssm_im": nrm((L, DEC_BATCH, G, N), 0.1),
        "state_conv": nrm((L, DEC_BATCH, CONV_K - 1, MIX_CONV), 1.0),
        "state_ffn_conv": nrm((L, DEC_BATCH, FFN_CONV_K - 1, D_FF), 1.0),
        "norm_mix": gain((L, D)),
        "w_in": nrm((L, D, MIX_SSM + 3 * MIX_CONV), D ** -0.5),
        "ssm_A_re": -0.5 + nrm((L, G, N), 0.01),
        "ssm_A_im": math.pi * n_idx + nrm((L, G, N), 0.01),
        "ssm_log_dt": jax.random.uniform(next(ks), (L, G), F32, math.log(1e-3), math.log(1e-1)),
        "ssm_B_re": nrm((L, G, N, P), (2 * P) ** -0.5),
        "ssm_B_im": nrm((L, G, N, P), (2 * P) ** -0.5),
        "ssm_C_re": nrm((L, G, P, N), (2 * N) ** -0.5),
        "ssm_C_im": nrm((L, G, P, N), (2 * N) ** -0.5),
        "ssm_D": nrm((L, G, P), 1.0),
        "w_glu": nrm((L, MIX_SSM, MIX_SSM), MIX_SSM ** -0.5),
        "b_glu": nrm((L, MIX_SSM), 0.02),
        "conv_w": nrm((L, CONV_K, MIX_CONV), CONV_K ** -0.5),
        "norm_ssm_out": gain((L, MIX_SSM)),
        "norm_conv_out": gain((L, MIX_CONV)),
        "w_out": nrm((L, D, D), D ** -0.5),
        "norm_xattn": gain((L, D)),
        "norm_mem": gain((L, D)),
        "w_q": nrm((L, D, XATTN_HEADS * XATTN_HEAD_DIM), D ** -0.5),
        "w_k": nrm((L, D, XATTN_HEADS * XATTN_HEAD_DIM), D ** -0.5),
        "w_v": nrm((L, D, XATTN_HEADS * XATTN_HEAD_DIM), D ** -0.5),
        "w_xo": nrm((L, XATTN_HEADS * XATTN_HEAD_DIM, D), D ** -0.5),
        "norm_ffn": gain((L, D)),
        "w_up": nrm((L, D, D_FF), D ** -0.5),
        "w_gate": nrm((L, D, D_FF), D ** -0.5),
        "ffn_conv_w": nrm((L, FFN_CONV_K, D_FF), FFN_CONV_K ** -0.5),
        "w_down": nrm((L, D_FF, D), D_FF ** -0.5),
        "norm_final": gain((D,)),
    }


def reference(x_prompt, x_sample, mem_prompt, cache_mem_k, cache_mem_v, state_ssm_re, state_ssm_im,
              state_conv, state_ffn_conv, norm_mix, w_in, ssm_A_re, ssm_A_im, ssm_log_dt, ssm_B_re,
              ssm_B_im, ssm_C_re, ssm_C_im, ssm_D, w_glu, b_glu, conv_w, norm_ssm_out, norm_conv_out,
              w_out, norm_xattn, norm_mem, w_q, w_k, w_v, w_xo, norm_ffn, w_up, w_gate, ffn_conv_w,
              w_down, norm_final):
    yp, ys = x_prompt, x_sample
    mk_p, mv_p, sre_p, sim_p, cb_p, fb_p = [], [], [], [], [], []
    sre_s, sim_s, cb_s, fb_s = [], [], [], []
    for l in range(DEPTH):
        p = dict(norm_mix=norm_mix[l], w_in=w_in[l], ssm_A_re=ssm_A_re[l], ssm_A_im=ssm_A_im[l],
                 ssm_log_dt=ssm_log_dt[l], ssm_B_re=ssm_B_re[l], ssm_B_im=ssm_B_im[l],
                 ssm_C_re=ssm_C_re[l], ssm_C_im=ssm_C_im[l], ssm_D=ssm_D[l], w_glu=w_glu[l],
                 b_glu=b_glu[l], conv_w=conv_w[l], norm_ssm_out=norm_ssm_out[l],
                 norm_conv_out=norm_conv_out[l], w_out=w_out[l], norm_xattn=norm_xattn[l],
                 w_q=w_q[l], w_xo=w_xo[l], norm_ffn=norm_ffn[l], w_up=w_up[l], w_gate=w_gate[l],
                 ffn_conv_w=ffn_conv_w[l], w_down=w_down[l])
        mk, mv = memory_kv(mem_prompt, norm_mem[l], w_k[l], w_v[l])
        yp, (a0, a1, a2, a3) = decoder_layer(yp, mk, mv, None, p)
        mk_p.append(mk); mv_p.append(mv)
        sre_p.append(a0); sim_p.append(a1); cb_p.append(a2); fb_p.append(a3)
        ys, (c0, c1, c2, c3) = decoder_layer(
            ys, cache_mem_k[l], cache_mem_v[l],
            (state_ssm_re[l], state_ssm_im[l], state_conv[l], state_ffn_conv[l]), p)
        sre_s.append(c0); sim_s.append(c1); cb_s.append(c2); fb_s.append(c3)
    yp = rms_norm(yp, norm_final)
    ys = rms_norm(ys, norm_final)
    return (yp, ys,
            jnp.stack(mk_p), jnp.stack(mv_p), jnp.stack(sre_p), jnp.stack(sim_p),
            jnp.stack(cb_p), jnp.stack(fb_p),
            jnp.stack(sre_s), jnp.stack(sim_s), jnp.stack(cb_s), jnp.stack(fb_s))
```

```python
import math
from contextlib import ExitStack
import numpy as np
import concourse.bass as bass
import concourse.mybir as mybir
from concourse.bass_utils import run_bass_kernel_spmd

F32 = mybir.dt.float32
BF16 = mybir.dt.bfloat16
I32 = mybir.dt.int32
AF = mybir.ActivationFunctionType
ALU = mybir.AluOpType

D = 1024
DFF = 2816
NMEM = 256
EPS = 1e-6
NCORES = 8
SEQ = 2048
NT = 512
NS = 64
BS = 16
TS = 4
BLK = 8


class Buf:
    __slots__ = ("name", "w", "r", "wd", "rd", "sem", "cnt")

    def __init__(self, name):
        self.name = name
        self.w = {}
        self.r = {}
        self.wd = []
        self.rd = []
        self.sem = None
        self.cnt = 0


class V:
    __slots__ = ("ap", "bufs", "kf")

    def __init__(self, ap, bufs, kf=None):
        self.ap = ap
        self.bufs = bufs
        self.kf = kf

    def __getitem__(self, k):
        return self.ap[k]

    def k(self, kt):
        if self.kf is None:
            return V(self.ap[:, kt], self.bufs)
        return V(self.ap[:, kt], self.kf(kt))

    def ks(self, k0, k1):
        if self.kf is None:
            return V(self.ap[:, k0:k1], self.bufs)
        b = []
        for kt in range(k0, k1):
            for x in self.kf(kt):
                if x not in b:
                    b.append(x)
        return V(self.ap[:, k0:k1], b)


class Op:
    __slots__ = ("eng", "fn", "deps", "dma", "sembuf", "dcnt", "signals", "sig", "raw")

    def __init__(self, eng, fn, deps, dma, sembuf):
        self.eng = eng
        self.fn = fn
        self.deps = deps
        self.dma = dma
        self.sembuf = sembuf
        self.dcnt = 0
        self.signals = False
        self.sig = 0
        self.raw = set()


class Sched:
    def __init__(self):
        self.ops = []
        self.stopped = False

    def add(self, eng, fn, r=(), w=(), dma=False, sembuf=None):
        if self.stopped:
            return -1
        idx = len(self.ops)
        rb = []
        for v in r:
            rb.extend(v.bufs if isinstance(v, V) else [v])
        wb = []
        for v in w:
            wb.extend(v.bufs if isinstance(v, V) else [v])
        deps = set()
        for b in rb:
            deps.update(b.w.values())
            deps.update(b.wd)
        raw = set(deps)
        for b in wb:
            deps.update(b.w.values())
            deps.update(b.wd)
            deps.update(b.r.values())
            deps.update(b.rd)
        op = Op(eng, fn, deps, dma, sembuf)
        op.raw = raw
        if dma:
            sembuf.cnt += 16
            op.dcnt = sembuf.cnt
        self.ops.append(op)
        for b in rb:
            if dma:
                b.rd.append(idx)
            else:
                b.r[eng] = idx
        for b in wb:
            if b.r or b.rd:
                only_self = (not b.rd or b.rd == [idx]) and all(x == idx for x in b.r.values())
                if not only_self or True:
                    b.r = {}
                    b.rd = []
                    b.w = {}
                    b.wd = []
            if dma:
                b.wd.append(idx)
            else:
                b.w[eng] = idx
        deps.discard(idx)
        return idx

    def finalize(self):
        for op in self.ops:
            for d in op.deps:
                dop = self.ops[d]
                if not dop.dma and (dop.eng != op.eng or (d in op.raw and op.eng in ("dve", "act", "pool"))):
                    dop.signals = True
        cnt = {}
        for op in self.ops:
            if op.dma:
                continue
            if op.signals:
                cnt[op.eng] = cnt.get(op.eng, 0) + 1
            op.sig = cnt.get(op.eng, 0)

    def replay(self, eng, e, engsem):
        waited = {}
        for op in self.ops:
            if op.eng != eng:
                continue
            need = {}
            for d in op.deps:
                dop = self.ops[d]
                if dop.dma:
                    s = dop.sembuf.sem
                    v = dop.dcnt
                elif dop.eng != eng or (d in op.raw and eng in ("dve", "act", "pool")):
                    s = engsem[dop.eng]
                    v = dop.sig
                else:
                    continue
                key = id(s)
                if key not in need or need[key][1] < v:
                    need[key] = (s, v)
            for key, (s, v) in need.items():
                if waited.get(key, 0) < v:
                    e.wait_ge(s, v)
                    waited[key] = v
            ins = op.fn(e)
            if op.dma:
                ins.then_inc(op.sembuf.sem, 16)
            elif op.signals:
                ins.then_inc(engsem[eng], 1)


class _Stop(Exception):
    pass


def build_program():
    import os
    LIMIT = float(os.environ.get("KSTAGE", "999"))

    def stage(n):
        if n >= LIMIT:
            S.stopped = True

    nc = bass.Bass("TRN2", target_bir_lowering=False)
    S = Sched()
    es = ExitStack()

    def din(name, shape):
        return nc.dram_tensor(name, list(shape), F32, kind="ExternalInput").ap()

    def dout(name, shape):
        return nc.dram_tensor(name, list(shape), F32, kind="ExternalOutput").ap()

    x_p = din("x_p", [SEQ, D])
    x_s = din("x_s", [NS, D])
    mem_p = din("mem_p", [NMEM, D])
    ck = din("ck", [BS, NMEM, D])
    cv = din("cv", [BS, NMEM, D])
    st_re = din("st_re", [BS, 2048])
    st_im = din("st_im", [BS, 2048])
    st_conv = din("st_conv", [BS * 2, 512])
    st_ffn = din("st_ffn", [BS * 2, DFF])
    norm_mix = din("norm_mix", [D])
    w_in = din("w_in", [D, 2048])
    A_re = din("A_re", [32, 64])
    A_im = din("A_im", [32, 64])
    log_dt = din("log_dt", [32])
    B_re = din("B_re", [32, 64, 16])
    B_im = din("B_im", [32, 64, 16])
    C_re = din("C_re", [32, 16, 64])
    C_im = din("C_im", [32, 16, 64])
    D_ssm = din("D_ssm", [512])
    w_glu = din("w_glu", [512, 512])
    b_glu = din("b_glu", [512])
    conv_w = din("conv_w", [3, 512])
    norm_ssm_out = din("norm_ssm_out", [512])
    norm_conv_out = din("norm_conv_out", [512])
    w_out = din("w_out", [D, D])
    norm_xattn = din("norm_xattn", [D])
    norm_mem = din("norm_mem", [D])
    w_q = din("w_q", [D, D])
    w_k = din("w_k", [D, D])
    w_v = din("w_v", [D, D])
    w_xo = din("w_xo", [D, D])
    norm_ffn = din("norm_ffn", [D])
    w_up = din("w_up", [D, DFF])
    w_gate = din("w_gate", [D, DFF])
    ffn_conv_w = din("ffn_conv_w", [3, DFF])
    w_down = din("w_down", [DFF, D])
    norm_final = din("norm_final", [D])

    y_p = dout("y_p", [SEQ, D])
    y_s = dout("y_s", [NS, D])
    mk_o = dout("mk_o", [NMEM, D])
    mv_o = dout("mv_o", [NMEM, D])
    sre_p = dout("sre_p", [16, 128])
    sim_p = dout("sim_p", [16, 128])
    cb_p = dout("cb_p", [2, 512])
    fb_p = dout("fb_p", [2, DFF])
    sre_s = dout("sre_s", [BS, 2048])
    sim_s = dout("sim_s", [BS, 2048])
    cb_s = dout("cb_s", [BS * 2, 512])
    fb_s = dout("fb_s", [BS * 2, DFF])
    DBG = os.environ.get("KDBG") is not None
    dbg_t = dout("dbg", [128, 8192]) if DBG else None
    dbg_off = [0]
    dbg_map = {}

    def sb(name, shape, dt, perk=False):
        t = es.enter_context(nc.sbuf_tensor(name, list(shape), dt))
        if perk:
            kb = [Buf(f"{name}_k{i}") for i in range(shape[1])]
            return V(t[:], kb, kf=lambda kt: [kb[kt]])
        return V(t[:], [Buf(name)])

    def sb_multi(name, shape, dt, n):
        t = es.enter_context(nc.sbuf_tensor(name, [shape[0], n] + list(shape[1:]), dt))
        return [V(t[:, i], [Buf(f"{name}{i}")]) for i in range(n)]

    PAGE = 256
    ARENA_PAGES = 88
    arena_t = es.enter_context(nc.sbuf_tensor("arena", [128, ARENA_PAGES * PAGE], F32))
    arena_pages = [Buf(f"pg{i}") for i in range(ARENA_PAGES)]
    arena_top = [0]

    def aalloc(shape_free, dt, at=None):
        n = 1
        for s in shape_free:
            n *= s
        n32 = n if dt != BF16 else (n + 1) // 2
        npg = (n32 + PAGE - 1) // PAGE
        if at is None:
            p0 = arena_top[0]
            assert p0 + npg <= ARENA_PAGES, ("arena overflow", p0, npg)
            arena_top[0] = p0 + npg
        else:
            p0 = at
        ap = arena_t[:, p0 * PAGE:p0 * PAGE + n32]
        if dt == BF16:
            ap = ap.bitcast(BF16)[:, 0:n]
        elif dt == I32:
            ap = ap.bitcast(I32)
        if len(shape_free) == 2:
            ap = ap.rearrange("p (a b) -> p a b", a=shape_free[0])
        elif len(shape_free) == 3:
            ap = ap.rearrange("p (a b c) -> p a b c", a=shape_free[0], b=shape_free[1])
        elif len(shape_free) == 4:
            ap = ap.rearrange("p (a b c d) -> p a b c d", a=shape_free[0], b=shape_free[1], c=shape_free[2])
        pages = arena_pages[p0:p0 + npg]
        kf = None
        if len(shape_free) >= 2:
            nk = shape_free[0]
            per = n32 / float(nk)

            def kf(kt, per=per, p0=p0, npg=npg):
                a = int(math.floor(kt * per / PAGE))
                b = int(math.ceil((kt + 1) * per / PAGE))
                return arena_pages[p0 + a:p0 + min(b, npg)]
        return V(ap, pages, kf=kf)

    def amark():
        return arena_top[0]

    def arelease(m):
        arena_top[0] = m

    psum = []
    for i in range(8):
        t = es.enter_context(nc.psum_tensor(f"ps{i}", [128, 512], F32))
        psum.append(V(t[:], [Buf(f"ps{i}")]))
    ps_i = [0]

    ps_avoid = []

    def psn():
        while True:
            v = psum[ps_i[0] % 8]
            ps_i[0] += 1
            if v not in ps_avoid:
                return v

    def mm(out, lhsT, rhs, start, stop, r, w, tp=None):
        if tp is None:
            S.add("pe", lambda e: e.matmul(out, lhsT, rhs, start=start, stop=stop), r, w)
        else:
            S.add("pe", lambda e: e.matmul(out, lhsT, rhs, start=start, stop=stop, tile_position=tp), r, w)

    def tr(out, in_, ident, r, w):
        S.add("pe", lambda e: e.transpose(out, in_, ident), r, w)

    def act(out, in_, func, r, w, scale=None, bias=None):
        kw = {}
        if scale is not None:
            kw["scale"] = scale
        if bias is not None:
            kw["bias"] = bias
        S.add("act", lambda e: e.activation(out, in_, func, **kw), r, w)

    def tt(eng, out, in0, in1, op, r, w):
        S.add(eng, lambda e: e.tensor_tensor(out, in0, in1, op), r, w)

    def ts(eng, out, in0, s1, s2, op0, op1, r, w):
        if s2 is None:
            S.add(eng, lambda e: e.tensor_scalar(out, in0, s1, None, op0), r, w)
        else:
            S.add(eng, lambda e: e.tensor_scalar(out, in0, s1, s2, op0, op1), r, w)

    def stt(eng, out, in0, scalar, in1, op0, op1, r, w):
        S.add(eng, lambda e: e.scalar_tensor_tensor(out, in0, scalar, in1, op0, op1), r, w)

    def cp(eng, out, in_, r, w):
        if eng == "act":
            S.add(eng, lambda e: e.activation(out, in_, AF.Identity), r, w)
        else:
            S.add(eng, lambda e: e.tensor_copy(out, in_), r, w)

    def recip(out, in_, r, w):
        S.add("dve", lambda e: e.reciprocal(out, in_), r, w)

    def memset(eng, v, val):
        S.add(eng, lambda e: e.memset(v.ap, val), [], [v])

    def dma(q, out, in_, r, w, sembuf, slow=False):
        if slow:
            S.add(q, lambda e: e.dma_start(out=out, in_=in_, allow_slow_non_contiguous=True), r, w, dma=True, sembuf=sembuf)
        else:
            S.add(q, lambda e: e.dma_start(out=out, in_=in_), r, w, dma=True, sembuf=sembuf)

    out_dma_bufs = []

    def dbg(name, v, ap=None):
        if not DBG:
            return
        ap = v.ap if ap is None else ap
        n = 1
        for d in ap.shape[1:]:
            n *= d
        o = dbg_off[0]
        dbg_off[0] += n
        dbg_map[name] = (o, tuple(ap.shape))
        dst = dbg_t[0:ap.shape[0], o:o + n]
        if len(ap.shape) == 3:
            dst = dst.rearrange("p (a b) -> p a b", a=ap.shape[1])
        elif len(ap.shape) == 4:
            dst = dst.rearrange("p (a b c) -> p a b c", a=ap.shape[1], b=ap.shape[2])
        b = v.bufs[0]
        if b not in out_dma_bufs:
            out_dma_bufs.append(b)
        dma("sp", dst, ap, [v], [], b)

    def dma_out(dram_ap, src_ap, srcv):
        b = srcv.bufs[0]
        if S.stopped:
            return
        if b not in out_dma_bufs:
            out_dma_bufs.append(b)
        dma("sp", dram_ap, src_ap, [srcv], [], b)

    ident_f = sb("ident_f", [128, 128], F32)
    ident_b = sb("ident_b", [128, 128], BF16)
    ones_b = sb("ones_b", [128, 128], BF16)
    zeros_f = sb("zeros_f", [128, 128], F32)
    maskpl = sb("maskpl", [128, 4], F32)
    parm16 = sb("parm16", [128, 2], F32)
    parm64 = sb("parm64", [128, 2], F32)
    nparm64 = sb("nparm64", [128, 2], F32)
    PA = sb("PA", [128, 68], F32)
    PB = sb("PB", [128, 66], F32)
    gains = V(PA.ap[:, 0:40].rearrange("p (a b) -> p a b", a=5), PA.bufs)
    g4 = V(PA.ap[:, 40:48].rearrange("p (a b) -> p a b", a=2), PA.bufs)
    bglu = V(PA.ap[:, 48:52], PA.bufs)
    Dcol = V(PA.ap[:, 52:56], PA.bufs)
    convw = V(PA.ap[:, 56:68].rearrange("p (k t) -> p t k", k=3), PA.bufs)
    fconvw = V(PB.ap.rearrange("p (k t) -> p t k", k=3), PB.bufs)
    Wz = sb("Wz", [128, 4, 8, 2, 128], BF16)
    Cj = sb("Cj", [128, 16, 8, 2, 32], BF16)
    KT = sb("KT", [128, 4, 8, 128], BF16)
    M12 = sb("M12", [128, 2, 2, 16], F32)
    a4 = sb("a4", [128, 2, 16], F32)
    R8 = sb("R8", [128, 16], F32)
    rotc = sb("rotc", [128, 16, NT // BLK], F32)
    rots = sb("rots", [128, 16, NT // BLK], F32)
    KTp = sb("KTp", [128, 8, 256], BF16)
    Vp = sb("Vp", [128, 2, 1024], BF16)
    xT = sb("xT", [128, 8, NT], F32, perk=True)
    hT = sb("hT", [128, 8, NT], BF16, perk=True)
    xst = sb_multi("xst", [128, 1024], F32, 2)
    sq = sb_multi("sq", [128, NT], BF16, 2)
    rs = sb("rs", [128, NT], F32)
    carry = sb("carry", [128, 3, 16], F32)
    cxh = sb("cxh", [128, 4, 2], F32)
    fhist_p = sb("fhist_p", [128, 22, 2], F32)
    WSLOT = 4096
    wpool_t = es.enter_context(nc.sbuf_tensor("wpool", [128, 3, WSLOT], BF16))
    wpool = [V(wpool_t[:, i], [Buf(f"wp{i}")]) for i in range(3)]
    wp_i = [0]

    NCHUNK = 48
    wsc = nc.dram_tensor("wsc", [NCHUNK, 128, WSLOT], BF16, kind="Internal").ap()
    wsc_reg = {}
    kvsc = nc.dram_tensor("kvsc", [2, BS, NMEM, D], BF16, kind="Internal").ap()
    kvsc_bufs = [Buf(f"kvsc{b}") for b in range(BS)]

    def convert_kv(b):
        for j, src in enumerate((ck, cv)):
            dma("pool", kvsc[j, b].rearrange("(m p) c -> p m c", p=128), src[b].rearrange("(m p) c -> p m c", p=128), [], [kvsc_bufs[b]], kvsc_bufs[b])
    wst_bufs = [Buf(f"wpst{i}") for i in range(3)]

    def load_w(dram2d, KTn, ncols, key=None):
        si = wp_i[0] % 3
        slot = wpool[si]
        wp_i[0] += 1
        n = KTn * ncols
        view = slot.ap[:, 0:n].rearrange("p (k c) -> p k c", k=KTn)
        if key is not None and key in wsc_reg:
            idx, scb = wsc_reg[key]
            dma("pool", slot.ap[:, 0:n], wsc[idx, :, 0:n], [scb], [slot], slot.bufs[0])
        else:
            dma("pool", view, dram2d.rearrange("(k p) c -> p k c", p=128), [], [slot], slot.bufs[0])
            if key is not None:
                idx = len(wsc_reg)
                assert idx < NCHUNK
                scb = Buf(f"wsc{idx}")
                wsc_reg[key] = (idx, scb)
                dma("sp", wsc[idx, :, 0:n], slot.ap[:, 0:n], [slot], [scb], wst_bufs[si])
        return V(view, slot.bufs)

    def norm_stage(src, KTn, Dn, gain_ap, dst, dst_k0, N, gv=None):
        pb = psn()
        for kt in range(KTn):
            sqv = sq[kt % 2]
            act(sqv.ap[:, :N], src.ap[:, kt, :N], AF.Square, [src.k(kt)], [sqv])
            mm(pb.ap[:, :N], ones_b.ap, sqv.ap[:, :N], kt == 0, kt == KTn - 1, [ones_b, sqv], [pb])
        act(rs.ap[:, :N], pb.ap[:, :N], AF.Ln, [pb], [rs], scale=1.0 / Dn, bias=EPS)
        act(rs.ap[:, :N], rs.ap[:, :N], AF.Exp, [rs], [rs], scale=-0.5)
        for kt in range(KTn):
            stt("dve", dst.ap[:, dst_k0 + kt, :N], src.ap[:, kt, :N], gain_ap[:, kt:kt + 1], rs.ap[:, :N],
                ALU.mult, ALU.mult, [src.k(kt), rs] + ([gv] if gv is not None else []), [dst.k(dst_k0 + kt)])

    def mm_stage(wdram, Kdim, cols, actv, N, evac, chunk=512, wname=None):
        KTn = Kdim // 128
        for (c0, nco) in cols:
            wv = load_w(wdram[:, c0:c0 + nco], KTn, nco, key=(wname, c0) if wname else None)
            for cti in range(nco // 128):
                pb = psn()
                for kt in range(KTn):
                    mm(pb.ap[:, :N], wv.ap[:, kt, cti * 128:(cti + 1) * 128], actv.ap[:, kt, :N],
                       kt == 0, kt == KTn - 1, [wv, actv.k(kt)], [pb])
                evac(c0 // 128 + cti, pb)

    def add_resid(N):
        def ev(ct, pb):
            tt("dve", xT.ap[:, ct, :N], pb.ap[:, :N], xT.ap[:, ct, :N], ALU.add, [pb, xT.k(ct)], [xT.k(ct)])
        return ev


    m0 = amark()
    idi = aalloc([128], I32)
    S.add("pool", lambda e: e.iota(idi.ap, pattern=[[1, 128]], base=0, channel_multiplier=-1), [], [idi])
    cp("dve", ident_f.ap, idi.ap, [idi], [ident_f])
    S.add("dve", lambda e: e.tensor_single_scalar(ident_f.ap, ident_f.ap, 0.0, ALU.is_equal), [ident_f], [ident_f])
    cp("dve", ident_b.ap, ident_f.ap, [ident_f], [ident_b])
    memset("dve", ones_b, 1.0)
    memset("dve", zeros_f, 0.0)
    memset("dve", carry, 0.0)
    memset("dve", cxh, 0.0)
    memset("dve", fhist_p, 0.0)
    AX = mybir.AxisListType.X
    S.add("dve", lambda e: e.reduce_sum(maskpl.ap, ident_f.ap.rearrange("p (a c) -> p a c", a=4), axis=AX), [ident_f], [maskpl])
    S.add("dve", lambda e: e.reduce_sum(parm64.ap, ident_f.ap.rearrange("p (a c) -> p a c", a=2), axis=AX), [ident_f], [parm64])
    tmp8 = aalloc([8], F32)
    S.add("dve", lambda e: e.reduce_sum(tmp8.ap, ident_f.ap.rearrange("p (a c) -> p a c", a=8), axis=AX), [ident_f], [tmp8])
    S.add("dve", lambda e: e.reduce_sum(parm16.ap, tmp8.ap.rearrange("p (a s) -> p s a", s=2), axis=AX), [tmp8], [parm16])
    ts("dve", nparm64.ap, parm64.ap, -1.0, None, ALU.mult, None, [parm64], [nparm64])

    stage(0)
    for s_ in range(2):
        dma("sp", xst[s_].ap, mem_p[s_ * 128:(s_ + 1) * 128, :], [], [xst[s_]], xst[s_].bufs[0])
    stA = aalloc([128], F32)
    stB = aalloc([128], F32)
    stL = aalloc([128], F32)
    stL2 = aalloc([128], F32)

    def ld_rows(st, r0, vec, nrow):
        dma("sp", st.ap[r0:r0 + nrow, :], vec.rearrange("(k p) -> k p", p=128), [], [st], st.bufs[0])

    r_ = 0
    for vec_ in (norm_mix, norm_xattn, norm_ffn, norm_final, norm_mem):
        ld_rows(stA, r_, vec_, 8)
        r_ += 8
    for vec_ in (norm_ssm_out, norm_conv_out, b_glu, D_ssm, conv_w[0], conv_w[1], conv_w[2]):
        ld_rows(stA, r_, vec_, 4)
        r_ += 4
    assert r_ == 68
    for k in range(3):
        ld_rows(stB, 22 * k, ffn_conv_w[k], 22)
    dma("sp", stL.ap[0:16, :], A_re.rearrange("(pr g) n -> pr (g n)", g=2), [], [stL], stL.bufs[0])
    dma("sp", stL2.ap[0:16, :], A_im.rearrange("(pr g) n -> pr (g n)", g=2), [], [stL2], stL2.bufs[0])
    pbA = psn()
    tr(pbA.ap[:, 0:68], stA.ap[0:68, :], ident_f.ap[0:68, 0:68], [stA, ident_f], [pbA])
    tr(pbA.ap[:, 128:194], stB.ap[0:66, :], ident_f.ap[0:66, 0:66], [stB, ident_f], [pbA])
    tr(pbA.ap[:, 256:272], stL.ap[0:16, :], ident_f.ap[0:16, 0:16], [stL, ident_f], [pbA])
    tr(pbA.ap[:, 272:288], stL2.ap[0:16, :], ident_f.ap[0:16, 0:16], [stL2, ident_f], [pbA])
    cp("act", PA.ap, pbA.ap[:, 0:68], [pbA], [PA])
    cp("act", PB.ap, pbA.ap[:, 128:194], [pbA], [PB])
    lr = aalloc([16], F32)
    li = aalloc([16], F32)
    ldt = aalloc([16], F32)
    Br = aalloc([16, 16], F32)
    Bi = aalloc([16, 16], F32)
    cp("act", lr.ap, pbA.ap[:, 256:272], [pbA], [lr])
    cp("act", li.ap, pbA.ap[:, 272:288], [pbA], [li])
    for gg in range(2):
        dma("sp", ldt.ap[64 * gg:64 * gg + 64, :], log_dt.rearrange("(pr g) -> g pr", g=2)[gg].partition_broadcast(64), [], [ldt], ldt.bufs[0], slow=True)
        dma("sp", Br.ap[64 * gg:64 * gg + 64], B_re.rearrange("(pr g) n p -> g n pr p", g=2)[gg], [], [Br], Br.bufs[0])
        dma("sp", Bi.ap[64 * gg:64 * gg + 64], B_im.rearrange("(pr g) n p -> g n pr p", g=2)[gg], [], [Bi], Bi.bufs[0])
    Cnat = aalloc([2, 4, 64], F32)
    dma("sp", Cnat.ap[:, 0], C_re.rearrange("(k g) q n -> (g q) k n", k=4), [], [Cnat], Cnat.bufs[0])
    dma("sp", Cnat.ap[:, 1], C_im.rearrange("(k g) q n -> (g q) k n", k=4), [], [Cnat], Cnat.bufs[0])
    m1 = amark()
    memT = aalloc([8, NMEM], F32)
    for s_ in range(2):
        st = xst[s_ % 2]
        for half in range(2):
            pb = psn()
            for j in range(4):
                kt = half * 4 + j
                tr(pb.ap[:, j * 128:(j + 1) * 128], st.ap[:, kt * 128:(kt + 1) * 128], ident_f.ap, [st, ident_f], [pb])
            cp("act" if half == 0 else "dve", memT.ap[:, half * 4:half * 4 + 4, s_ * 128:(s_ + 1) * 128],
               pb.ap.rearrange("p (k c) -> p k c", k=4), [pb], [memT])
    stage(2.2)
    mhT = aalloc([8, NMEM], BF16)
    norm_stage(memT, 8, D, gains.ap[:, 4, :], mhT, 0, NMEM, gv=gains)
    stage(2.4)
    kvst = [aalloc([512], F32) for _ in range(2)]
    kv_i = [0]
    for (wd, outd, isk) in ((w_k, mk_o, True), (w_v, mv_o, False)):
        for cc in range(2):
            wv = load_w(wd[:, cc * 512:(cc + 1) * 512], 8, 512)
            stage(2.5)
            if isk:
                for cti in range(4):
                    pb = psn()
                    for kt in range(8):
                        mm(pb.ap[:, :NMEM], wv.ap[:, kt, cti * 128:(cti + 1) * 128], mhT.ap[:, kt, :], kt == 0, kt == 7, [wv, mhT], [pb])
                    act(KTp.ap[:, cc * 4 + cti, :], pb.ap[:, :NMEM], AF.Identity, [pb], [KTp])
            stage(2.6)
            for mt in range(2):
                pb = psn()
                for kt in range(8):
                    mm(pb.ap, mhT.ap[:, kt, mt * 128:(mt + 1) * 128], wv.ap[:, kt, :], kt == 0, kt == 7, [wv, mhT], [pb])
                stg = kvst[kv_i[0] % 2]
                kv_i[0] += 1
                cp("act", stg.ap, pb.ap, [pb], [stg])
                if not isk:
                    act(Vp.ap[:, mt, cc * 512:(cc + 1) * 512], stg.ap, AF.Identity, [stg], [Vp])
                stage(2.65)
                dma_out(outd[mt * 128:(mt + 1) * 128, cc * 512:(cc + 1) * 512], stg.ap, stg)
                stage(2.7 + 0.01 * kv_i[0])
    stage(3)
    stage(1)
    Cnp = aalloc([2, 4, 2, 64], F32)
    for ri in range(2):
        for s_ in range(2):
            ts("dve", Cnp.ap[:, ri, :, s_, :], Cnat.ap[:, ri], parm16.ap[:, s_:s_ + 1], None, ALU.mult, None, [Cnat, parm16], [Cnp])
    Cpad = aalloc([2, 4, 128], F32)
    for ri in range(2):
        pb = psn()
        for kt in range(4):
            tr(pb.ap[:, kt * 128:(kt + 1) * 128], Cnp.ap[:, ri, kt].rearrange("p s n -> p (s n)"), ident_f.ap, [Cnp, ident_f], [pb])
        if ri == 0:
            cp("dve", Cpad.ap[:, 0].rearrange("p k c -> p (k c)"), pb.ap, [pb], [Cpad])
        else:
            ts("dve", Cpad.ap[:, 1].rearrange("p k c -> p (k c)"), pb.ap, -1.0, None, ALU.mult, None, [pb], [Cpad])

    sc = aalloc([24, 16], F32)
    def SC(i):
        return sc.ap[:, i, :]
    dtv, mag, ph, kf, r_, sn, shf, cs, ar, ai, den, rden, cr, ci, t1, t2, t3 = [SC(i) for i in range(17)]
    ki = aalloc([16], I32)
    scv = [sc]
    act(dtv, ldt.ap, AF.Exp, [ldt], scv)
    dlr = SC(17)
    tt("dve", dlr, dtv, lr.ap, ALU.mult, scv + [lr], scv)
    tt("dve", ph, dtv, li.ap, ALU.mult, scv + [li], scv)
    act(mag, dlr, AF.Exp, scv, scv)
    act(R8.ap, dlr, AF.Exp, scv, [R8], scale=8.0)
    act(SC(18), dlr, AF.Exp, scv, scv, scale=-8.0)
    ts("dve", kf, ph, 1.0 / (2 * math.pi), None, ALU.mult, None, scv, scv)
    cp("dve", ki.ap, kf, scv, [ki])
    cp("dve", kf, ki.ap, [ki], scv)
    stt("dve", r_, kf, -2 * math.pi, ph, ALU.mult, ALU.add, scv, scv)
    ts("dve", r_, r_, 3.14159, -3.14159, ALU.min, ALU.max, scv, scv)
    act(sn, r_, AF.Sin, scv, scv)
    act(shf, r_, AF.Sin, scv, scv, scale=0.5)
    tt("dve", cs, shf, shf, ALU.mult, scv, scv)
    ts("dve", cs, cs, -2.0, 1.0, ALU.mult, ALU.add, scv, scv)
    tt("dve", ar, mag, cs, ALU.mult, scv, scv)
    tt("dve", ai, mag, sn, ALU.mult, scv, scv)
    tt("dve", den, lr.ap, lr.ap, ALU.mult, scv + [lr], scv)
    tt("dve", t1, li.ap, li.ap, ALU.mult, scv + [li], scv)
    tt("dve", den, den, t1, ALU.add, scv, scv)
    recip(rden, den, scv, scv)
    ts("dve", t1, ar, -1.0, None, ALU.add, None, scv, scv)
    tt("dve", t2, t1, lr.ap, ALU.mult, scv + [lr], scv)
    tt("dve", t3, ai, li.ap, ALU.mult, scv + [li], scv)
    tt("dve", t2, t2, t3, ALU.add, scv, scv)
    tt("dve", cr, t2, rden, ALU.mult, scv, scv)
    tt("dve", t2, ai, lr.ap, ALU.mult, scv + [lr], scv)
    tt("dve", t3, t1, li.ap, ALU.mult, scv + [li], scv)
    tt("dve", t2, t2, t3, ALU.subtract, scv, scv)
    tt("dve", ci, t2, rden, ALU.mult, scv, scv)

    def bc(ap16):
        return ap16.unsqueeze(2).to_broadcast([128, 16, 16])

    def bc2(ap16, n):
        return ap16.unsqueeze(2).to_broadcast([128, 16, n])

    bb = aalloc([2, 16, 16], F32)
    tb = aalloc([16, 16], F32)
    tt("dve", bb.ap[:, 0], Br.ap, bc(cr), ALU.mult, [Br] + scv, [bb])
    tt("dve", tb.ap, Bi.ap, bc(ci), ALU.mult, [Bi] + scv, [tb])
    tt("dve", bb.ap[:, 0], bb.ap[:, 0], tb.ap, ALU.subtract, [bb, tb], [bb])
    tt("dve", bb.ap[:, 1], Bi.ap, bc(cr), ALU.mult, [Bi] + scv, [bb])
    tt("dve", tb.ap, Br.ap, bc(ci), ALU.mult, [Br] + scv, [tb])
    tt("dve", bb.ap[:, 1], bb.ap[:, 1], tb.ap, ALU.add, [bb, tb], [bb])
    pw = aalloc([9, 2, 16], F32)
    S.add("dve", lambda e: e.memset(pw.ap[:, 0, 0, :], 1.0), [], [pw])
    S.add("dve", lambda e: e.memset(pw.ap[:, 0, 1, :], 0.0), [], [pw])
    cp("dve", pw.ap[:, 1, 0, :], ar, scv, [pw])
    cp("dve", pw.ap[:, 1, 1, :], ai, scv, [pw])
    for k in range(2, 9):
        tt("dve", pw.ap[:, k, 0, :], pw.ap[:, k - 1, 0, :], ar, ALU.mult, [pw] + scv, [pw])
        tt("dve", t1, pw.ap[:, k - 1, 1, :], ai, ALU.mult, [pw] + scv, scv)
        tt("dve", pw.ap[:, k, 0, :], pw.ap[:, k, 0, :], t1, ALU.subtract, [pw] + scv, [pw])
        tt("dve", pw.ap[:, k, 1, :], pw.ap[:, k - 1, 0, :], ai, ALU.mult, [pw] + scv, [pw])
        tt("dve", t1, pw.ap[:, k - 1, 1, :], ar, ALU.mult, [pw] + scv, scv)
        tt("dve", pw.ap[:, k, 1, :], pw.ap[:, k, 1, :], t1, ALU.add, [pw] + scv, [pw])
    cp("dve", M12.ap[:, 0, 0, :], pw.ap[:, 8, 0, :], [pw], [M12])
    cp("dve", M12.ap[:, 0, 1, :], pw.ap[:, 8, 0, :], [pw], [M12])
    ts("dve", M12.ap[:, 1, 0, :], pw.ap[:, 8, 1, :], -1.0, None, ALU.mult, None, [pw], [M12])
    cp("dve", M12.ap[:, 1, 1, :], pw.ap[:, 8, 1, :], [pw], [M12])
    cp("dve", a4.ap, pw.ap[:, 4], [pw], [a4])
    NBr = NT // BLK
    cur_r, cur_i, nx_r, nx_i, tq1 = SC(19), SC(20), SC(21), SC(22), SC(23)
    tt("pool", cur_r, pw.ap[:, 8, 0, :], SC(18), ALU.mult, [pw] + scv, scv)
    tt("pool", cur_i, pw.ap[:, 8, 1, :], SC(18), ALU.mult, [pw] + scv, scv)
    cp("pool", rotc.ap[:, :, 0], cur_r, scv, [rotc])
    cp("pool", rots.ap[:, :, 0], cur_i, scv, [rots])
    tBr = aalloc([16, NBr // 2], F32)
    m_ = 1
    while m_ < NBr:
        br_ = cur_r.unsqueeze(2).to_broadcast([128, 16, m_])
        bi_ = cur_i.unsqueeze(2).to_broadcast([128, 16, m_])
        tbv = tBr.ap[:, :, 0:m_]
        tt("pool", rotc.ap[:, :, m_:2 * m_], rotc.ap[:, :, 0:m_], br_, ALU.mult, [rotc] + scv, [rotc])
        tt("pool", tbv, rots.ap[:, :, 0:m_], bi_, ALU.mult, [rots] + scv, [tBr])
        tt("pool", rotc.ap[:, :, m_:2 * m_], rotc.ap[:, :, m_:2 * m_], tbv, ALU.subtract, [rotc, tBr], [rotc])
        tt("pool", rots.ap[:, :, m_:2 * m_], rotc.ap[:, :, 0:m_], bi_, ALU.mult, [rotc] + scv, [rots])
        tt("pool", tbv, rots.ap[:, :, 0:m_], br_, ALU.mult, [rots] + scv, [tBr])
        tt("pool", rots.ap[:, :, m_:2 * m_], rots.ap[:, :, m_:2 * m_], tbv, ALU.add, [rots, tBr], [rots])
        m_ *= 2
        if m_ < NBr:
            tt("pool", nx_r, cur_r, cur_r, ALU.mult, scv, scv)
            tt("pool", tq1, cur_i, cur_i, ALU.mult, scv, scv)
            tt("pool", nx_r, nx_r, tq1, ALU.subtract, scv, scv)
            tt("pool", nx_i, cur_r, cur_i, ALU.mult, scv, scv)
            ts("pool", nx_i, nx_i, 2.0, None, ALU.mult, None, scv, scv)
            cp("pool", cur_r, nx_r, scv, scv)
            cp("pool", cur_i, nx_i, scv, scv)

    CpR = Cpad.ap[:, 0].rearrange("p k (a c) -> p (k a) c", a=4)
    CpNI = Cpad.ap[:, 1].rearrange("p k (a c) -> p (k a) c", a=4)
    tf = aalloc([2, 16, 32], F32)
    for k in range(1, 9):
        pr_ = bc2(pw.ap[:, k, 0, :], 32)
        pi_ = bc2(pw.ap[:, k, 1, :], 32)
        tt("pool", tf.ap[:, 0], CpR, pr_, ALU.mult, [Cpad, pw], [tf])
        tt("pool", tf.ap[:, 1], CpNI, pi_, ALU.mult, [Cpad, pw], [tf])
        tt("pool", Cj.ap[:, :, k - 1, 0, :], tf.ap[:, 0], tf.ap[:, 1], ALU.add, [tf], [Cj])
        tt("pool", tf.ap[:, 0], CpNI, pr_, ALU.mult, [Cpad, pw], [tf])
        tt("pool", tf.ap[:, 1], CpR, pi_, ALU.mult, [Cpad, pw], [tf])
        tt("pool", Cj.ap[:, :, k - 1, 1, :], tf.ap[:, 0], tf.ap[:, 1], ALU.subtract, [tf], [Cj])

    Ek = aalloc([2, 16, 16], F32)
    Epads = [aalloc([2, 16, 2, 16], F32) for _ in range(2)]
    diagD = aalloc([4, 128], F32)
    for kt in range(4):
        ts("dve", diagD.ap[:, kt, :], ident_f.ap, Dcol.ap[:, kt:kt + 1], None, ALU.mult, None, [ident_f, Dcol], [diagD])
    for k in range(8):
        Ep = Epads[k % 2]
        pr_ = bc(pw.ap[:, k, 0, :])
        pi_ = bc(pw.ap[:, k, 1, :])
        tt("dve", Ek.ap[:, 0], bb.ap[:, 0], pr_, ALU.mult, [bb, pw], [Ek])
        tt("dve", tb.ap, bb.ap[:, 1], pi_, ALU.mult, [bb, pw], [tb])
        tt("dve", Ek.ap[:, 0], Ek.ap[:, 0], tb.ap, ALU.subtract, [Ek, tb], [Ek])
        tt("dve", Ek.ap[:, 1], bb.ap[:, 1], pr_, ALU.mult, [bb, pw], [Ek])
        tt("dve", tb.ap, bb.ap[:, 0], pi_, ALU.mult, [bb, pw], [tb])
        tt("dve", Ek.ap[:, 1], Ek.ap[:, 1], tb.ap, ALU.add, [Ek, tb], [Ek])
        for ri in range(2):
            for s_ in range(2):
                ts("dve", Ep.ap[:, ri, :, s_, :], Ek.ap[:, ri], parm64.ap[:, s_:s_ + 1], None, ALU.mult, None, [Ek, parm64], [Ep])
        for ri in range(2):
            pb = psn()
            for kt in range(4):
                tr(pb.ap[:, kt * 128:(kt + 1) * 128],
                   Ep.ap[:, ri, 4 * kt:4 * kt + 4].rearrange("p a s c -> p (a s c)"), ident_f.ap, [Ep, ident_f], [pb])
            cp("act", Wz.ap[:, :, 7 - k, ri, :], pb.ap.rearrange("p (k c) -> p k c", k=4), [pb], [Wz])
        pb = psn()
        for z4 in range(4):
            mm(pb.ap[:, z4 * 128:(z4 + 1) * 128], zeros_f.ap, zeros_f.ap, True, False, [zeros_f], [pb])
        for kt in range(4):
            for pl in range(4):
                pair = 4 * kt + pl
                for ri in range(2):
                    last = (kt == 3 and pl == 3 and ri == 1)
                    mm(pb.ap[32 * pl:32 * pl + 32, kt * 128 + 32 * pl:kt * 128 + 32 * pl + 32],
                       Ep.ap[:, ri, pair].rearrange("p s c -> p (s c)"),
                       Cpad.ap[:, ri, kt, 32 * pl:32 * pl + 32], False, last, [Ep, Cpad], [pb], tp=(0, 32 * pl))
        if k == 0:
            tt("dve", KT.ap[:, :, 0, :], pb.ap.rearrange("p (k c) -> p k c", k=4), diagD.ap, ALU.add, [pb, diagD], [KT])
        else:
            cp("act", KT.ap[:, :, k, :], pb.ap.rearrange("p (k c) -> p k c", k=4), [pb], [KT])
    dbg("sc", sc)
    dbg("pw", pw)
    dbg("bb", bb)
    dbg("lr", lr)
    dbg("li", li)
    dbg("ldt", ldt)
    dbg("a4", a4)
    dbg("M12", M12)
    dbg("Cpad", Cpad)
    dbg("parm64", parm64)
    dbg("parm16", parm16)
    dbg("maskpl", maskpl)
    build_program.dbg_map = dbg_map
    arelease(m0)
    stage(2)

    xstage = aalloc([4, 1024], F32, at=72)

    def prefetch_x(ti):
        for s_ in range(4):
            v = xstage.k(s_)
            dma("sp", v.ap, x_p[ti * NT + s_ * 128: ti * NT + (s_ + 1) * 128, :], [], [v], v.bufs[0])

    def run_tile(kind, ti):
        prompt = kind == "p"
        N = NT if prompt else NS
        nb, T = (1, NT) if prompt else (BS, TS)
        last_p = prompt and ti == SEQ // NT - 1
        mt0_ = amark()
        if LIMIT > 3:
            S.stopped = False
        NKV = 6
        kvbufs = []
        if not prompt:
            for j in range(NKV):
                kr = aalloc([2, 1024], BF16)
                vb_ = aalloc([2, 1024], BF16)
                kvbufs.append((kr, vb_))
                dma("pool", kr.ap, kvsc[0, j].rearrange("(m p) c -> p m c", p=128), [kvsc_bufs[j]], [kr], kr.bufs[0])
                dma("pool", vb_.ap, kvsc[1, j].rearrange("(m p) c -> p m c", p=128), [kvsc_bufs[j]], [vb_], vb_.bufs[0])
        mt_ = amark()

        if prompt:
            if ti == 0:
                prefetch_x(0)
            for s_ in range(4):
                st = xstage.k(s_)
                for half in range(2):
                    pb = psn()
                    for j in range(4):
                        kt = half * 4 + j
                        tr(pb.ap[:, j * 128:(j + 1) * 128], st.ap[:, kt * 128:(kt + 1) * 128], ident_f.ap, [st, ident_f], [pb])
                    cp("act" if half == 0 else "dve", xT.ap[:, half * 4:half * 4 + 4, s_ * 128:(s_ + 1) * 128],
                       pb.ap.rearrange("p (k c) -> p k c", k=4), [pb], [xT.ks(half * 4, half * 4 + 4)])
        else:
            st = xst[0]
            dma("sp", st.ap[0:NS, :], x_s, [], [st], st.bufs[0])
            for half in range(2):
                pb = psn()
                for j in range(4):
                    kt = half * 4 + j
                    tr(pb.ap[:, j * NS:(j + 1) * NS], st.ap[0:NS, kt * 128:(kt + 1) * 128], ident_f.ap[0:NS, 0:NS], [st, ident_f], [pb])
                cp("act" if half == 0 else "dve", xT.ap[:, half * 4:half * 4 + 4, 0:NS],
                   pb.ap[:, 0:4 * NS].rearrange("p (k c) -> p k c", k=4), [pb], [xT.ks(half * 4, half * 4 + 4)])

        stage(4)
        norm_stage(xT, 8, D, gains.ap[:, 0, :], hT, 0, N, gv=gains)
        umask = aalloc([4, 4, N], BF16)
        ufull = aalloc([4, N], BF16)
        cx_p0 = amark()
        cx = aalloc([4, nb, T + 2], F32)
        cvb_p0 = amark()
        cvb = aalloc([4, nb, T], F32)
        mixedT = aalloc([8, N], BF16)
        if prompt:
            cp("dve", cx.ap[:, :, 0, 0:2], cxh.ap, [cxh], [cx])
        else:
            scs = aalloc([512], F32)
            dma("sp", scs.ap[0:32, :], st_conv, [], [scs], scs.bufs[0])
            pb = psn()
            for kt in range(4):
                tr(pb.ap[:, kt * 32:(kt + 1) * 32], scs.ap[0:32, kt * 128:(kt + 1) * 128], ident_f.ap[0:32, 0:32], [scs, ident_f], [pb])
            cp("dve", cx.ap[:, :, :, 0:2], pb.ap[:, 0:128].rearrange("p (k b c) -> p k b c", k=4, b=BS), [pb], [cx])

        def ev_win(ct, pb):
            if ct < 4:
                cp("dve", ufull.ap[:, ct, :], pb.ap[:, :N], [pb], [ufull.k(ct)])
                for pl in range(4):
                    if pl < 2:
                        act(umask.ap[:, ct, pl, :], ufull.ap[:, ct, :], AF.Identity, [ufull.k(ct), maskpl], [umask.k(ct)], scale=maskpl.ap[:, pl:pl + 1])
                    else:
                        ts("dve", umask.ap[:, ct, pl, :], ufull.ap[:, ct, :], maskpl.ap[:, pl:pl + 1], None, ALU.mult, None, [ufull.k(ct), maskpl], [umask.k(ct)])
            elif ct < 8:
                act(cx.ap[:, ct - 4, :, 2:2 + T], pb.ap[:, :N].rearrange("p (b t) -> p b t", b=nb), AF.Identity, [pb], [cx])
            elif ct >= 12:
                kt = ct - 12
                tt("dve", cx.ap[:, kt, :, 2:2 + T], pb.ap[:, :N].rearrange("p (b t) -> p b t", b=nb),
                   cx.ap[:, kt, :, 2:2 + T], ALU.mult, [pb, cx], [cx])
                ts("dve", cvb.ap[:, kt], cx.ap[:, kt, :, 2:2 + T], convw.ap[:, kt, 2:3], None, ALU.mult, None, [cx, convw], [cvb])
                stt("dve", cvb.ap[:, kt], cx.ap[:, kt, :, 1:1 + T], convw.ap[:, kt, 1:2], cvb.ap[:, kt], ALU.mult, ALU.add, [cx, convw, cvb], [cvb])
                stt("dve", cvb.ap[:, kt], cx.ap[:, kt, :, 0:T], convw.ap[:, kt, 0:1], cvb.ap[:, kt], ALU.mult, ALU.add, [cx, convw, cvb], [cvb])
            else:
                kt = ct - 8
                tt("dve", cvb.ap[:, kt], pb.ap[:, :N].rearrange("p (b t) -> p b t", b=nb), cvb.ap[:, kt], ALU.mult, [pb, cvb], [cvb])

        ssmA = {}

        def ssm_part_a():
            NB = NT // BLK
            Sbf = aalloc([2, 16, NB], BF16)
            Wt = aalloc([2, 16, NB], F32)
            Vt = aalloc([2, 16, NB], F32)
            T12 = aalloc([2, 16, NB], F32)
            zb = [[psn(), psn()] for _ in range(2)]
            for pair in range(16):
                kt, pl = pair // 4, pair % 4
                for ri in range(2):
                    zv = zb[ri][pair // 8]
                    for i in range(BLK):
                        mm(zv.ap[:, (pair % 8) * NB:(pair % 8 + 1) * NB], Wz.ap[:, kt, i, ri, :],
                           umask.ap[:, kt, pl, :].rearrange("p (c j) -> p c j", j=BLK)[:, :, i],
                           i == 0, i == BLK - 1, [Wz, umask.k(kt)], [zv])
            for hf in range(2):
                zr = zb[0][hf].ap.rearrange("p (a c) -> p a c", a=8)
                zi = zb[1][hf].ap.rearrange("p (a c) -> p a c", a=8)
                c_ = rotc.ap[:, hf * 8:(hf + 1) * 8, :]
                s_ = rots.ap[:, hf * 8:(hf + 1) * 8, :]
                t1 = T12.ap[:, 0, hf * 8:(hf + 1) * 8, :]
                t2 = T12.ap[:, 1, hf * 8:(hf + 1) * 8, :]
                wr = Wt.ap[:, 0, hf * 8:(hf + 1) * 8, :]
                wi = Wt.ap[:, 1, hf * 8:(hf + 1) * 8, :]
                tt("dve", t1, zr, c_, ALU.mult, [zb[0][hf], rotc], [T12])
                tt("dve", t2, zi, s_, ALU.mult, [zb[1][hf], rots], [T12])
                tt("dve", wr, t1, t2, ALU.add, [T12], [Wt])
                tt("dve", t1, zi, c_, ALU.mult, [zb[1][hf], rotc], [T12])
                tt("dve", t2, zr, s_, ALU.mult, [zb[0][hf], rots], [T12])
                tt("dve", wi, t1, t2, ALU.subtract, [T12], [Wt])
            for ri in range(2):
                for pair in range(16):
                    S.add("dve", lambda e, ri=ri, pair=pair: e.tensor_tensor_scan(
                        Vt.ap[:, ri, pair, :], R8.ap[:, pair:pair + 1].to_broadcast([128, NB]), Wt.ap[:, ri, pair, :],
                        carry.ap[:, ri, pair:pair + 1], ALU.mult, ALU.add), [R8, Wt, carry], [Vt])
            cp("dve", Sbf.ap[:, :, :, 0], carry.ap[:, 0:2, :], [carry], [Sbf])
            t1 = T12.ap[:, 0]
            t2 = T12.ap[:, 1]
            tt("dve", t1, rotc.ap, Vt.ap[:, 0], ALU.mult, [rotc, Vt], [T12])
            tt("dve", t2, rots.ap, Vt.ap[:, 1], ALU.mult, [rots, Vt], [T12])
            tt("dve", Sbf.ap[:, 0, :, 1:NB], t1[:, :, 0:NB - 1], t2[:, :, 0:NB - 1], ALU.subtract, [T12], [Sbf])
            tt("dve", carry.ap[:, 0, :], t1[:, :, NB - 1], t2[:, :, NB - 1], ALU.subtract, [T12], [carry])
            tt("dve", t1, rotc.ap, Vt.ap[:, 1], ALU.mult, [rotc, Vt], [T12])
            tt("dve", t2, rots.ap, Vt.ap[:, 0], ALU.mult, [rots, Vt], [T12])
            tt("dve", Sbf.ap[:, 1, :, 1:NB], t1[:, :, 0:NB - 1], t2[:, :, 0:NB - 1], ALU.add, [T12], [Sbf])
            tt("dve", carry.ap[:, 1, :], t1[:, :, NB - 1], t2[:, :, NB - 1], ALU.add, [T12], [carry])
            ssmA["Sbf"] = Sbf

        mm_stage(w_in, D, [(0, 512), (512, 512), (1536, 512)], hT, N, ev_win, wname="w_in")
        if prompt:
            ssm_part_a()
        mm_stage(w_in, D, [(1024, 512)], hT, N, ev_win, wname="w_in")

        if prompt:
            cp("dve", cxh.ap, cx.ap[:, :, 0, T:T + 2], [cx], [cxh])
            if last_p:
                pb = psn()
                for kt in range(4):
                    tr(pb.ap[0:2, kt * 128:(kt + 1) * 128], cxh.ap[:, kt, :], ident_f.ap, [cxh, ident_f], [pb])
                ostg = aalloc([512], F32)
                cp("dve", ostg.ap[0:2, :], pb.ap[0:2, :], [pb], [ostg])
                dma_out(cb_p, ostg.ap[0:2, :], ostg)
        else:
            cl = aalloc([4, BS, 2], F32)
            cp("dve", cl.ap, cx.ap[:, :, :, T:T + 2], [cx], [cl])
            pb = psn()
            for kt in range(4):
                tr(pb.ap[0:32, kt * 128:(kt + 1) * 128], cl.ap[:, kt].rearrange("p b c -> p (b c)"), ident_f.ap, [cl, ident_f], [pb])
            ostg = aalloc([512], F32)
            cp("dve", ostg.ap[0:32, :], pb.ap[0:32, :], [pb], [ostg])
            dma_out(cb_s, ostg.ap[0:32, :], ostg)
        cvf = V(cvb.ap.rearrange("p k b t -> p k (b t)"), cvb.bufs)
        norm_stage(cvf, 4, 512, g4.ap[:, 1, :], mixedT, 4, N, gv=g4)

        stage(5)
        g1 = aalloc([4, N], F32, at=cx_p0)
        g1b = aalloc([4, N], BF16)
        if prompt:
            NB = NT // BLK
            Sbf = ssmA["Sbf"]
            stage(5.6)
            for kt in range(4):
                yb = psn()
                y3 = yb.ap.rearrange("p (c j) -> p c j", j=BLK)
                u3 = ufull.ap[:, kt, :].rearrange("p (c j) -> p c j", j=BLK)
                for tau in range(BLK):
                    mm(y3[:, :, tau:BLK], KT.ap[:, kt, tau, :], u3[:, :, 0:BLK - tau], tau == 0, False, [KT, ufull.k(kt)], [yb])
                for pl in range(4):
                    pair = 4 * kt + pl
                    for ri in range(2):
                        for j in range(BLK):
                            last = (pl == 3 and ri == 1 and j == BLK - 1)
                            mm(y3[32 * pl:32 * pl + 32, :, j], Cj.ap[:, pair, j, ri, :], Sbf.ap[:, ri, pair, :],
                               False, last, [Cj, Sbf], [yb], tp=(0, 32 * pl))
                act(g1.ap[:, kt, :], yb.ap, AF.Gelu_apprx_tanh, [yb], [g1.k(kt)])
                cp("dve", g1b.ap[:, kt, :], g1.ap[:, kt, :], [g1.k(kt)], [g1b.k(kt)])
            stage(5.8)
            if last_p:
                pb = psn()
                for ri in range(2):
                    tr(pb.ap[0:16, ri * 128:(ri + 1) * 128], carry.ap[:, ri, :], ident_f.ap, [carry, ident_f], [pb])
                sst = aalloc([256], F32)
                cp("dve", sst.ap[0:16, :], pb.ap[0:16, 0:256], [pb], [sst])
                dma_out(sre_p, sst.ap[0:16, 0:128], sst)
                dma_out(sim_p, sst.ap[0:16, 128:256], sst)
        else:
            sstg = aalloc([2, 2048], F32)
            dma("sp", sstg.ap[0:BS, 0, :], st_re, [], [sstg], sstg.bufs[0])
            dma("sp", sstg.ap[0:BS, 1, :], st_im, [], [sstg], sstg.bufs[0])
            S0 = aalloc([2, 16, BS], F32)
            S0b = aalloc([2, 16, BS], BF16)
            for ri in range(2):
                pb = psn()
                for pair in range(16):
                    tr(pb.ap[:, pair * BS:(pair + 1) * BS], sstg.ap[0:BS, ri, pair * 128:(pair + 1) * 128],
                       ident_f.ap[0:BS, 0:BS], [sstg, ident_f], [pb])
                cp("dve", S0.ap[:, ri].rearrange("p a b -> p (a b)"), pb.ap[:, 0:256], [pb], [S0])
                act(S0b.ap[:, ri].rearrange("p a b -> p (a b)"), S0.ap[:, ri].rearrange("p a b -> p (a b)"), AF.Identity, [S0], [S0b])
            for kt in range(4):
                yb = psn()
                y3 = yb.ap[:, :N].rearrange("p (b t) -> p b t", t=TS)
                u3 = ufull.ap[:, kt, :].rearrange("p (b t) -> p b t", t=TS)
                for tau in range(TS):
                    mm(y3[:, :, tau:TS], KT.ap[:, kt, tau, :], u3[:, :, 0:TS - tau], tau == 0, False, [KT, ufull.k(kt)], [yb])
                for pl in range(4):
                    pair = 4 * kt + pl
                    for ri in range(2):
                        for j in range(TS):
                            last = (pl == 3 and ri == 1 and j == TS - 1)
                            mm(y3[32 * pl:32 * pl + 32, :, j], Cj.ap[:, pair, j, ri, :], S0b.ap[:, ri, pair, :],
                               False, last, [Cj, S0b], [yb], tp=(0, 32 * pl))
                act(g1.ap[:, kt, :], yb.ap[:, :N], AF.Gelu_apprx_tanh, [yb], [g1.k(kt)])
                cp("dve", g1b.ap[:, kt, :], g1.ap[:, kt, :], [g1.k(kt)], [g1b.k(kt)])
            zb = [psn(), psn()]
            for pair in range(16):
                kt, pl = pair // 4, pair % 4
                for ri in range(2):
                    for i in range(TS):
                        mm(zb[ri].ap[:, pair * BS:(pair + 1) * BS], Wz.ap[:, kt, i + 4, ri, :],
                           umask.ap[:, kt, pl, :].rearrange("p (b t) -> p b t", t=TS)[:, :, i],
                           i == 0, i == TS - 1, [Wz, umask.k(kt)], [zb[ri]])
            S3 = aalloc([2, 16, BS], F32)
            tq = aalloc([16, BS], F32)
            a4r = a4.ap[:, 0, :].unsqueeze(2).to_broadcast([128, 16, BS])
            a4i = a4.ap[:, 1, :].unsqueeze(2).to_broadcast([128, 16, BS])
            z0 = zb[0].ap[:, 0:256].rearrange("p (a b) -> p a b", a=16)
            z1 = zb[1].ap[:, 0:256].rearrange("p (a b) -> p a b", a=16)
            tt("dve", tq.ap, S0.ap[:, 0], a4r, ALU.mult, [S0, a4], [tq])
            tt("dve", S3.ap[:, 0], z0, tq.ap, ALU.add, [zb[0], tq], [S3])
            tt("dve", tq.ap, S0.ap[:, 1], a4i, ALU.mult, [S0, a4], [tq])
            tt("dve", S3.ap[:, 0], S3.ap[:, 0], tq.ap, ALU.subtract, [S3, tq], [S3])
            tt("dve", tq.ap, S0.ap[:, 1], a4r, ALU.mult, [S0, a4], [tq])
            tt("dve", S3.ap[:, 1], z1, tq.ap, ALU.add, [zb[1], tq], [S3])
            tt("dve", tq.ap, S0.ap[:, 0], a4i, ALU.mult, [S0, a4], [tq])
            tt("dve", S3.ap[:, 1], S3.ap[:, 1], tq.ap, ALU.add, [S3, tq], [S3])
            for ri in range(2):
                for q4 in range(4):
                    pb = psn()
                    for j in range(4):
                        pair = q4 * 4 + j
                        tr(pb.ap[0:BS, j * 128:(j + 1) * 128], S3.ap[:, ri, pair, :], ident_f.ap, [S3, ident_f], [pb])
                    cp("act" if q4 % 2 == 0 else "dve", sstg.ap[0:BS, ri, q4 * 512:(q4 + 1) * 512], pb.ap[0:BS, :], [pb], [sstg])
            dma_out(sre_s, sstg.ap[0:BS, 0, :], sstg)
            dma_out(sim_s, sstg.ap[0:BS, 1, :], sstg)

        stage(6)
        sg = aalloc([4, N], F32, at=cvb_p0)

        def ev_glu(ct, pb):
            act(sg.ap[:, ct, :], pb.ap[:, :N], AF.Sigmoid, [pb, bglu], [sg.k(ct)], bias=bglu.ap[:, ct:ct + 1])
            tt("dve", sg.ap[:, ct, :], sg.ap[:, ct, :], g1.ap[:, ct, :], ALU.mult, [sg.k(ct), g1.k(ct)], [sg.k(ct)])

        mm_stage(w_glu, 512, [(0, 512)], g1b, N, ev_glu, wname="w_glu")
        norm_stage(sg, 4, 512, g4.ap[:, 0, :], mixedT, 0, N, gv=g4)
        mm_stage(w_out, D, [(0, 512), (512, 512)], mixedT, N, add_resid(N), wname="w_out")
        arelease(mt_)

        stage(7)
        norm_stage(xT, 8, D, gains.ap[:, 1, :], hT, 0, N, gv=gains)
        qT = aalloc([8, N], BF16)
        oT = aalloc([8, N], BF16)

        def ev_q(ct, pb):
            act(qT.ap[:, ct, :], pb.ap[:, :N], AF.Identity, [pb], [qT.k(ct)], scale=1.0 / 16.0)

        mm_stage(w_q, D, [(0, 512), (512, 512)], hT, N, ev_q, wname="w_q")
        if prompt:
            PT = [aalloc([2, N], BF16) for _ in range(2)]
            rsa = [aalloc([N], F32) for _ in range(2)]
            for h in range(4):
                Pv = PT[h % 2]
                for mt in range(2):
                    pb = psn()
                    for dt_ in range(2):
                        mm(pb.ap, KTp.ap[:, 2 * h + dt_, mt * 128:(mt + 1) * 128], qT.ap[:, 2 * h + dt_, :],
                           dt_ == 0, dt_ == 1, [KTp, qT.k(2 * h + dt_)], [pb])
                    act(Pv.ap[:, mt, :], pb.ap, AF.Exp, [pb], [Pv])
                pb = psn()
                for mt in range(2):
                    mm(pb.ap, ones_b.ap, Pv.ap[:, mt, :], mt == 0, mt == 1, [ones_b, Pv], [pb])
                rv = rsa[h % 2]
                recip(rv.ap, pb.ap, [pb], [rv])
                for dt_ in range(2):
                    pb = psn()
                    for mt in range(2):
                        mm(pb.ap, Vp.ap[:, mt, h * 256 + dt_ * 128:h * 256 + (dt_ + 1) * 128], Pv.ap[:, mt, :],
                           mt == 0, mt == 1, [Vp, Pv], [pb])
                    tt("dve", oT.ap[:, 2 * h + dt_, :], pb.ap, rv.ap, ALU.mult, [pb, rv], [oT.k(2 * h + dt_)])
        else:
            KTb = [aalloc([8, NMEM], BF16) for _ in range(2)]
            PTb = [aalloc([8, TS], BF16) for _ in range(2)]
            ob = psn()
            sm = psn()
            ps_avoid.extend([ob, sm])
            def s_load_tr(b):
                kr, vb_ = kvbufs[b % NKV]
                ktb = KTb[b % 2]
                if b >= NKV:
                    dma("pool", kr.ap, kvsc[0, b].rearrange("(m p) c -> p m c", p=128), [kvsc_bufs[b]], [kr], kr.bufs[0])
                    dma("pool", vb_.ap, kvsc[1, b].rearrange("(m p) c -> p m c", p=128), [kvsc_bufs[b]], [vb_], vb_.bufs[0])
                for mt in range(2):
                    pb = psn()
                    pbb = pb.ap.bitcast(BF16)
                    for hd in range(8):
                        tr(pbb[:, hd * 128:(hd + 1) * 128], kr.ap[:, mt, hd * 128:(hd + 1) * 128], ident_b.ap, [kr, ident_b], [pb])
                    cp("act" if mt == 0 else "dve", ktb.ap[:, :, mt * 128:(mt + 1) * 128],
                       pbb.rearrange("p (k c) -> p k c", k=8), [pb], [ktb])

            def s_scores(b):
                ktb, ptb = KTb[b % 2], PTb[b % 2]
                pb = psn()
                for h in range(4):
                    for mt in range(2):
                        for dt_ in range(2):
                            mm(pb.ap[:, (2 * h + mt) * TS:(2 * h + mt + 1) * TS], ktb.ap[:, 2 * h + dt_, mt * 128:(mt + 1) * 128],
                               qT.ap[:, 2 * h + dt_, b * TS:(b + 1) * TS], dt_ == 0, dt_ == 1, [ktb, qT], [pb])
                act(ptb.ap.rearrange("p a t -> p (a t)"), pb.ap[:, 0:8 * TS], AF.Exp, [pb], [ptb])

            def s_pv(b):
                kr, vb_ = kvbufs[b % NKV]
                ptb = PTb[b % 2]
                for h in range(4):
                    for dt_ in range(2):
                        for mt in range(2):
                            mm(ob.ap[:, (2 * h + dt_) * NS + b * TS:(2 * h + dt_) * NS + (b + 1) * TS],
                               vb_.ap[:, mt, h * 256 + dt_ * 128:h * 256 + (dt_ + 1) * 128], ptb.ap[:, 2 * h + mt, :],
                               mt == 0, mt == 1, [vb_, ptb], [ob])
                    for mt in range(2):
                        mm(sm.ap[:, h * NS + b * TS:h * NS + (b + 1) * TS], ones_b.ap, ptb.ap[:, 2 * h + mt, :],
                           mt == 0, mt == 1, [ones_b, ptb], [sm])

            s_load_tr(0)
            for b in range(BS):
                s_scores(b)
                if b + 1 < BS:
                    s_load_tr(b + 1)
                s_pv(b)
            del ps_avoid[:]
            rs4 = aalloc([4, NS], F32)
            recip(rs4.ap.rearrange("p a b -> p (a b)"), sm.ap[:, 0:4 * NS], [sm], [rs4])
            for dt_ in range(2):
                tt("dve", oT.ap.rearrange("p (h d) n -> p h d n", d=2)[:, :, dt_, :],
                   ob.ap.rearrange("p (h d n) -> p h d n", h=4, d=2)[:, :, dt_, :], rs4.ap, ALU.mult, [ob, rs4], [oT])
        mm_stage(w_xo, D, [(0, 512), (512, 512)], oT, N, add_resid(N), wname="w_xo")
        arelease(mt_)

        stage(8)
        if prompt and not last_p:
            prefetch_x(ti + 1)
        if prompt and ti >= 1:
            for b_ in {1: range(0, 5), 2: range(5, 10), 3: range(10, 16)}[ti]:
                convert_kv(b_)
        norm_stage(xT, 8, D, gains.ap[:, 2, :], hT, 0, N, gv=gains)
        ffT = aalloc([22, N], BF16)
        upc = [aalloc([nb, T + 2], F32) for _ in range(2)]
        av = [aalloc([nb, T], F32) for _ in range(2)]
        if prompt:
            fh = V(fhist_p.ap.rearrange("p k (b c) -> p k b c", b=1), fhist_p.bufs)
        else:
            fh = aalloc([22, BS, 2], F32)
            fstg = aalloc([DFF], F32)
            dma("sp", fstg.ap[0:32, :], st_ffn, [], [fstg], fstg.bufs[0])
            for q6 in range(6):
                pb = psn()
                nct = min(4, 22 - q6 * 4)
                for j in range(nct):
                    ct = q6 * 4 + j
                    tr(pb.ap[:, j * 32:(j + 1) * 32], fstg.ap[0:32, ct * 128:(ct + 1) * 128], ident_f.ap[0:32, 0:32], [fstg, ident_f], [pb])
                cp("dve", fh.ap[:, q6 * 4:q6 * 4 + nct].rearrange("p k b c -> p k (b c)"),
                   pb.ap[:, 0:nct * 32].rearrange("p (k c) -> p k c", k=nct), [pb], [fh])
        up_i = [0]
        ffn_cols = [(c0, min(512, DFF - c0)) for c0 in range(0, DFF, 512)]
        for (c0, nco) in ffn_cols:
            wu = load_w(w_up[:, c0:c0 + nco], 8, nco, key=("w_up", c0))
            wg = load_w(w_gate[:, c0:c0 + nco], 8, nco, key=("w_gate", c0))
            for cti in range(nco // 128):
                ct = c0 // 128 + cti
                pu = psn()
                for kt in range(8):
                    mm(pu.ap[:, :N], wu.ap[:, kt, cti * 128:(cti + 1) * 128], hT.ap[:, kt, :N], kt == 0, kt == 7, [wu, hT.k(kt)], [pu])
                pg = psn()
                for kt in range(8):
                    mm(pg.ap[:, :N], wg.ap[:, kt, cti * 128:(cti + 1) * 128], hT.ap[:, kt, :N], kt == 0, kt == 7, [wg, hT.k(kt)], [pg])
                uc = upc[up_i[0] % 2]
                a_ = av[up_i[0] % 2]
                up_i[0] += 1
                cp("dve", uc.ap[:, :, 0:2], fh.ap[:, ct], [fh], [uc])
                act(uc.ap[:, :, 2:2 + T], pu.ap[:, :N].rearrange("p (b t) -> p b t", b=nb), AF.Identity, [pu], [uc])
                cp("dve", fh.ap[:, ct], uc.ap[:, :, T:T + 2], [uc], [fh])
                ts("dve", a_.ap, uc.ap[:, :, 2:2 + T], fconvw.ap[:, ct, 2:3], None, ALU.mult, None, [uc, fconvw], [a_])
                stt("dve", a_.ap, uc.ap[:, :, 1:1 + T], fconvw.ap[:, ct, 1:2], a_.ap, ALU.mult, ALU.add, [uc, fconvw, a_], [a_])
                stt("dve", a_.ap, uc.ap[:, :, 0:T], fconvw.ap[:, ct, 0:1], a_.ap, ALU.mult, ALU.add, [uc, fconvw, a_], [a_])
                act(a_.ap, a_.ap, AF.Gelu_apprx_tanh, [a_], [a_])
                tt("dve", ffT.ap[:, ct, :], pg.ap[:, :N], a_.ap.rearrange("p b t -> p (b t)"), ALU.mult, [pg, a_], [ffT.k(ct)])
        if last_p or not prompt:
            nrow = 2 if prompt else 32
            fo = aalloc([DFF], F32)
            for q6 in range(6):
                pb = psn()
                nct = min(4, 22 - q6 * 4)
                for j in range(nct):
                    ct = q6 * 4 + j
                    tr(pb.ap[0:nrow, j * 128:(j + 1) * 128], fh.ap[:, ct].rearrange("p b c -> p (b c)"), ident_f.ap, [fh, ident_f], [pb])
                cp("act" if q6 % 2 == 0 else "dve", fo.ap[0:nrow, q6 * 512:q6 * 512 + nct * 128], pb.ap[0:nrow, 0:nct * 128], [pb], [fo])
            dma_out(fb_p if prompt else fb_s, fo.ap[0:nrow, :], fo)
        mm_stage(w_down, DFF, [(c * 128, 128) for c in range(8)], ffT, N, add_resid(N), wname="w_down")
        arelease(mt_)

        stage(9)
        pb = psn()
        for kt in range(8):
            sqv = sq[kt % 2]
            act(sqv.ap[:, :N], xT.ap[:, kt, :N], AF.Square, [xT.k(kt)], [sqv])
            mm(pb.ap[:, :N], ones_b.ap, sqv.ap[:, :N], kt == 0, kt == 7, [ones_b, sqv], [pb])
        act(rs.ap[:, :N], pb.ap[:, :N], AF.Ln, [pb], [rs], scale=1.0 / D, bias=EPS)
        act(rs.ap[:, :N], rs.ap[:, :N], AF.Exp, [rs], [rs], scale=-0.5)
        for kt in range(8):
            stt("dve", xT.ap[:, kt, :N], xT.ap[:, kt, :N], gains.ap[:, 3, kt:kt + 1], rs.ap[:, :N], ALU.mult, ALU.mult, [xT.k(kt), rs, gains], [xT.k(kt)])
        if prompt:
            for s_ in range(4):
                st = xst[s_ % 2]
                for half in range(2):
                    pb = psn()
                    for j in range(4):
                        kt = half * 4 + j
                        tr(pb.ap[:, j * 128:(j + 1) * 128], xT.ap[:, kt, s_ * 128:(s_ + 1) * 128], ident_f.ap, [xT.k(kt), ident_f], [pb])
                    cp("act" if half == 0 else "dve", st.ap[:, half * 512:(half + 1) * 512], pb.ap, [pb], [st])
                dma_out(y_p[ti * NT + s_ * 128: ti * NT + (s_ + 1) * 128, :], st.ap, st)
        else:
            st = xst[0]
            for half in range(2):
                pb = psn()
                for j in range(4):
                    kt = half * 4 + j
                    tr(pb.ap[0:NS, j * 128:(j + 1) * 128], xT.ap[:, kt, 0:NS], ident_f.ap, [xT.k(kt), ident_f], [pb])
                cp("act" if half == 0 else "dve", st.ap[0:NS, half * 512:(half + 1) * 512], pb.ap[0:NS, :], [pb], [st])
            dma_out(y_s, st.ap[0:NS, :], st)
        arelease(mt0_)

    for ti in range(SEQ // NT):
        run_tile("p", ti)
    run_tile("s", 0)

    S.stopped = False
    S.add("sp", lambda e: e.nop(), [], out_dma_bufs)

    S.finalize()
    engsem = {}
    for name in ("pe", "act", "dve", "pool", "sp"):
        engsem[name] = es.enter_context(nc.semaphore("sem_" + name))
    allbufs = set()
    for op in S.ops:
        if op.dma:
            allbufs.add(op.sembuf)
    for i, b in enumerate(sorted(allbufs, key=lambda b: b.name)):
        b.sem = es.enter_context(nc.semaphore(f"dsem{i}"))
    with nc.Block() as block:
        @block.sync
        def _(e):
            S.replay("sp", e, engsem)

        @block.gpsimd
        def _(e):
            S.replay("pool", e, engsem)

        @block.scalar
        def _(e):
            S.replay("act", e, engsem)

        @block.vector
        def _(e):
            S.replay("dve", e, engsem)

        @block.tensor
        def _(e):
            S.replay("pe", e, engsem)
    es.close()
    return nc


_NC_CACHE = {}


def kernel(x_prompt, x_sample, mem_prompt, cache_mem_k, cache_mem_v, state_ssm_re, state_ssm_im,
           state_conv, state_ffn_conv, norm_mix, w_in, ssm_A_re, ssm_A_im, ssm_log_dt, ssm_B_re,
           ssm_B_im, ssm_C_re, ssm_C_im, ssm_D, w_glu, b_glu, conv_w, norm_ssm_out, norm_conv_out,
           w_out, norm_xattn, norm_mem, w_q, w_k, w_v, w_xo, norm_ffn, w_up, w_gate, ffn_conv_w,
           w_down, norm_final):
    f = lambda a: np.ascontiguousarray(np.asarray(a, dtype=np.float32))
    if "nc" not in _NC_CACHE:
        _NC_CACHE["nc"] = build_program()
    nc = _NC_CACHE["nc"]
    shared = {
        "norm_mix": f(norm_mix[0]), "w_in": f(w_in[0]), "A_re": f(ssm_A_re[0]), "A_im": f(ssm_A_im[0]),
        "log_dt": f(ssm_log_dt[0]), "B_re": f(ssm_B_re[0]), "B_im": f(ssm_B_im[0]), "C_re": f(ssm_C_re[0]),
        "C_im": f(ssm_C_im[0]), "D_ssm": f(ssm_D[0]).reshape(512), "w_glu": f(w_glu[0]), "b_glu": f(b_glu[0]),
        "conv_w": f(conv_w[0]), "norm_ssm_out": f(norm_ssm_out[0]), "norm_conv_out": f(norm_conv_out[0]),
        "w_out": f(w_out[0]), "norm_xattn": f(norm_xattn[0]), "norm_mem": f(norm_mem[0]), "w_q": f(w_q[0]),
        "w_k": f(w_k[0]), "w_v": f(w_v[0]), "w_xo": f(w_xo[0]), "norm_ffn": f(norm_ffn[0]), "w_up": f(w_up[0]),
        "w_gate": f(w_gate[0]), "ffn_conv_w": f(ffn_conv_w[0]), "w_down": f(w_down[0]), "norm_final": f(norm_final),
    }
    in_maps = []
    for c in range(NCORES):
        sl = slice(c * BS, (c + 1) * BS)
        m = dict(shared)
        m["x_p"] = f(x_prompt[c])
        m["x_s"] = f(x_sample[sl]).reshape(NS, D)
        m["mem_p"] = f(mem_prompt[c])
        m["ck"] = f(cache_mem_k[0, sl]).reshape(BS, NMEM, D)
        m["cv"] = f(cache_mem_v[0, sl]).reshape(BS, NMEM, D)
        m["st_re"] = f(state_ssm_re[0, sl]).reshape(BS, 2048)
        m["st_im"] = f(state_ssm_im[0, sl]).reshape(BS, 2048)
        m["st_conv"] = f(state_conv[0, sl]).reshape(BS * 2, 512)
        m["st_ffn"] = f(state_ffn_conv[0, sl]).reshape(BS * 2, DFF)
        in_maps.append(m)
    res = run_bass_kernel_spmd(nc, in_maps, core_ids=list(range(NCORES)))
    R = res.results
    kernel.last_results = R
    cat = lambda k: np.concatenate([np.asarray(r[k], dtype=np.float32) for r in R], axis=0)
    stk = lambda k: np.stack([np.asarray(r[k], dtype=np.float32) for r in R], axis=0)
    yp = stk("y_p")
    ys = cat("y_s").reshape(NCORES * BS, TS, D)
    mk = stk("mk_o").reshape(1, NCORES, NMEM, 4, 256)
    mv = stk("mv_o").reshape(1, NCORES, NMEM, 4, 256)
    srp = stk("sre_p").reshape(1, NCORES, 32, 64)
    sip = stk("sim_p").reshape(1, NCORES, 32, 64)
    cbp = stk("cb_p").reshape(1, NCORES, 2, 512)
    fbp = stk("fb_p").reshape(1, NCORES, 2, DFF)
    srs = cat("sre_s").reshape(1, NCORES * BS, 32, 64)
    sis = cat("sim_s").reshape(1, NCORES * BS, 32, 64)
    cbs = cat("cb_s").reshape(1, NCORES * BS, 2, 512)
    fbs = cat("fb_s").reshape(1, NCORES * BS, 2, DFF)
    return (yp, ys, mk, mv, srp, sip, cbp, fbp, srs, sis, cbs, fbs)
```

```python
import math
from contextlib import ExitStack
import numpy as np
import concourse.bass as bass
import concourse.mybir as mybir
from concourse.bass_utils import run_bass_kernel_spmd

F32 = mybir.dt.float32
BF16 = mybir.dt.bfloat16
I32 = mybir.dt.int32
AF = mybir.ActivationFunctionType
ALU = mybir.AluOpType

D = 1024
DFF = 2816
NMEM = 256
EPS = 1e-6
NCORES = 8
SEQ = 2048
NT = 512
NS = 64
BS = 16
TS = 4
BLK = 8


class Buf:
    __slots__ = ("name", "w", "r", "wd", "rd", "sem", "cnt")

    def __init__(self, name):
        self.name = name
        self.w = {}
        self.r = {}
        self.wd = []
        self.rd = []
        self.sem = None
        self.cnt = 0


class V:
    __slots__ = ("ap", "bufs", "kf")

    def __init__(self, ap, bufs, kf=None):
        self.ap = ap
        self.bufs = bufs
        self.kf = kf

    def __getitem__(self, k):
        return self.ap[k]

    def k(self, kt):
        if self.kf is None:
            return V(self.ap[:, kt], self.bufs)
        return V(self.ap[:, kt], self.kf(kt))

    def ks(self, k0, k1):
        if self.kf is None:
            return V(self.ap[:, k0:k1], self.bufs)
        b = []
        for kt in range(k0, k1):
            for x in self.kf(kt):
                if x not in b:
                    b.append(x)
        return V(self.ap[:, k0:k1], b)


class Op:
    __slots__ = ("eng", "fn", "deps", "dma", "sembuf", "dcnt", "signals", "sig", "raw")

    def __init__(self, eng, fn, deps, dma, sembuf):
        self.eng = eng
        self.fn = fn
        self.deps = deps
        self.dma = dma
        self.sembuf = sembuf
        self.dcnt = 0
        self.signals = False
        self.sig = 0
        self.raw = set()


class Sched:
    def __init__(self):
        self.ops = []
        self.stopped = False

    def add(self, eng, fn, r=(), w=(), dma=False, sembuf=None):
        if self.stopped:
            return -1
        idx = len(self.ops)
        rb = []
        for v in r:
            rb.extend(v.bufs if isinstance(v, V) else [v])
        wb = []
        for v in w:
            wb.extend(v.bufs if isinstance(v, V) else [v])
        deps = set()
        for b in rb:
            deps.update(b.w.values())
            deps.update(b.wd)
        raw = set(deps)
        for b in wb:
            deps.update(b.w.values())
            deps.update(b.wd)
            deps.update(b.r.values())
            deps.update(b.rd)
        op = Op(eng, fn, deps, dma, sembuf)
        op.raw = raw
        if dma:
            sembuf.cnt += 16
            op.dcnt = sembuf.cnt
        self.ops.append(op)
        for b in rb:
            if dma:
                b.rd.append(idx)
            else:
                b.r[eng] = idx
        for b in wb:
            if b.r or b.rd:
                only_self = (not b.rd or b.rd == [idx]) and all(x == idx for x in b.r.values())
                if not only_self or True:
                    b.r = {}
                    b.rd = []
                    b.w = {}
                    b.wd = []
            if dma:
                b.wd.append(idx)
            else:
                b.w[eng] = idx
        deps.discard(idx)
        return idx

    def finalize(self):
        for op in self.ops:
            for d in op.deps:
                dop = self.ops[d]
                if not dop.dma and (dop.eng != op.eng or (d in op.raw and op.eng in ("dve", "act", "pool"))):
                    dop.signals = True
        cnt = {}
        for op in self.ops:
            if op.dma:
                continue
            if op.signals:
                cnt[op.eng] = cnt.get(op.eng, 0) + 1
            op.sig = cnt.get(op.eng, 0)

    def replay(self, eng, e, engsem):
        waited = {}
        for op in self.ops:
            if op.eng != eng:
                continue
            need = {}
            for d in op.deps:
                dop = self.ops[d]
                if dop.dma:
                    s = dop.sembuf.sem
                    v = dop.dcnt
                elif dop.eng != eng or (d in op.raw and eng in ("dve", "act", "pool")):
                    s = engsem[dop.eng]
                    v = dop.sig
                else:
                    continue
                key = id(s)
                if key not in need or need[key][1] < v:
                    need[key] = (s, v)
            for key, (s, v) in need.items():
                if waited.get(key, 0) < v:
                    e.wait_ge(s, v)
                    waited[key] = v
            ins = op.fn(e)
            if op.dma:
                ins.then_inc(op.sembuf.sem, 16)
            elif op.signals:
                ins.then_inc(engsem[eng], 1)


class _Stop(Exception):
    pass


def build_program():
    import os
    LIMIT = float(os.environ.get("KSTAGE", "999"))

    def stage(n):
        if n >= LIMIT:
            S.stopped = True

    nc = bass.Bass("TRN2", target_bir_lowering=False)
    S = Sched()
    es = ExitStack()

    def din(name, shape):
        return nc.dram_tensor(name, list(shape), F32, kind="ExternalInput").ap()

    def dout(name, shape):
        return nc.dram_tensor(name, list(shape), F32, kind="ExternalOutput").ap()

    x_p = din("x_p", [SEQ, D])
    x_s = din("x_s", [NS, D])
    mem_p = din("mem_p", [NMEM, D])
    ck = din("ck", [BS, NMEM, D])
    cv = din("cv", [BS, NMEM, D])
    st_re = din("st_re", [BS, 2048])
    st_im = din("st_im", [BS, 2048])
    st_conv = din("st_conv", [BS * 2, 512])
    st_ffn = din("st_ffn", [BS * 2, DFF])
    norm_mix = din("norm_mix", [D])
    w_in = din("w_in", [D, 2048])
    A_re = din("A_re", [32, 64])
    A_im = din("A_im", [32, 64])
    log_dt = din("log_dt", [32])
    B_re = din("B_re", [32, 64, 16])
    B_im = din("B_im", [32, 64, 16])
    C_re = din("C_re", [32, 16, 64])
    C_im = din("C_im", [32, 16, 64])
    D_ssm = din("D_ssm", [512])
    w_glu = din("w_glu", [512, 512])
    b_glu = din("b_glu", [512])
    conv_w = din("conv_w", [3, 512])
    norm_ssm_out = din("norm_ssm_out", [512])
    norm_conv_out = din("norm_conv_out", [512])
    w_out = din("w_out", [D, D])
    norm_xattn = din("norm_xattn", [D])
    norm_mem = din("norm_mem", [D])
    w_q = din("w_q", [D, D])
    w_k = din("w_k", [D, D])
    w_v = din("w_v", [D, D])
    w_xo = din("w_xo", [D, D])
    norm_ffn = din("norm_ffn", [D])
    w_up = din("w_up", [D, DFF])
    w_gate = din("w_gate", [D, DFF])
    ffn_conv_w = din("ffn_conv_w", [3, DFF])
    w_down = din("w_down", [DFF, D])
    norm_final = din("norm_final", [D])

    y_p = dout("y_p", [SEQ, D])
    y_s = dout("y_s", [NS, D])
    mk_o = dout("mk_o", [NMEM, D])
    mv_o = dout("mv_o", [NMEM, D])
    sre_p = dout("sre_p", [16, 128])
    sim_p = dout("sim_p", [16, 128])
    cb_p = dout("cb_p", [2, 512])
    fb_p = dout("fb_p", [2, DFF])
    sre_s = dout("sre_s", [BS, 2048])
    sim_s = dout("sim_s", [BS, 2048])
    cb_s = dout("cb_s", [BS * 2, 512])
    fb_s = dout("fb_s", [BS * 2, DFF])
    DBG = os.environ.get("KDBG") is not None
    dbg_t = dout("dbg", [128, 8192]) if DBG else None
    dbg_off = [0]
    dbg_map = {}

    def sb(name, shape, dt, perk=False):
        t = es.enter_context(nc.sbuf_tensor(name, list(shape), dt))
        if perk:
            kb = [Buf(f"{name}_k{i}") for i in range(shape[1])]
            return V(t[:], kb, kf=lambda kt: [kb[kt]])
        return V(t[:], [Buf(name)])

    def sb_multi(name, shape, dt, n):
        t = es.enter_context(nc.sbuf_tensor(name, [shape[0], n] + list(shape[1:]), dt))
        return [V(t[:, i], [Buf(f"{name}{i}")]) for i in range(n)]

    PAGE = 256
    ARENA_PAGES = 88
    arena_t = es.enter_context(nc.sbuf_tensor("arena", [128, ARENA_PAGES * PAGE], F32))
    arena_pages = [Buf(f"pg{i}") for i in range(ARENA_PAGES)]
    arena_top = [0]

    def aalloc(shape_free, dt, at=None):
        n = 1
        for s in shape_free:
            n *= s
        n32 = n if dt != BF16 else (n + 1) // 2
        npg = (n32 + PAGE - 1) // PAGE
        if at is None:
            p0 = arena_top[0]
            assert p0 + npg <= ARENA_PAGES, ("arena overflow", p0, npg)
            arena_top[0] = p0 + npg
        else:
            p0 = at
        ap = arena_t[:, p0 * PAGE:p0 * PAGE + n32]
        if dt == BF16:
            ap = ap.bitcast(BF16)[:, 0:n]
        elif dt == I32:
            ap = ap.bitcast(I32)
        if len(shape_free) == 2:
            ap = ap.rearrange("p (a b) -> p a b", a=shape_free[0])
        elif len(shape_free) == 3:
            ap = ap.rearrange("p (a b c) -> p a b c", a=shape_free[0], b=shape_free[1])
        elif len(shape_free) == 4:
            ap = ap.rearrange("p (a b c d) -> p a b c d", a=shape_free[0], b=shape_free[1], c=shape_free[2])
        pages = arena_pages[p0:p0 + npg]
        kf = None
        if len(shape_free) >= 2:
            nk = shape_free[0]
            per = n32 / float(nk)

            def kf(kt, per=per, p0=p0, npg=npg):
                a = int(math.floor(kt * per / PAGE))
                b = int(math.ceil((kt + 1) * per / PAGE))
                return arena_pages[p0 + a:p0 + min(b, npg)]
        return V(ap, pages, kf=kf)

    def amark():
        return arena_top[0]

    def arelease(m):
        arena_top[0] = m

    psum = []
    for i in range(8):
        t = es.enter_context(nc.psum_tensor(f"ps{i}", [128, 512], F32))
        psum.append(V(t[:], [Buf(f"ps{i}")]))
    ps_i = [0]

    ps_avoid = []

    def psn():
        while True:
            v = psum[ps_i[0] % 8]
            ps_i[0] += 1
            if v not in ps_avoid:
                return v

    def mm(out, lhsT, rhs, start, stop, r, w, tp=None):
        if tp is None:
            S.add("pe", lambda e: e.matmul(out, lhsT, rhs, start=start, stop=stop), r, w)
        else:
            S.add("pe", lambda e: e.matmul(out, lhsT, rhs, start=start, stop=stop, tile_position=tp), r, w)

    def tr(out, in_, ident, r, w):
        S.add("pe", lambda e: e.transpose(out, in_, ident), r, w)

    def act(out, in_, func, r, w, scale=None, bias=None):
        kw = {}
        if scale is not None:
            kw["scale"] = scale
        if bias is not None:
            kw["bias"] = bias
        S.add("act", lambda e: e.activation(out, in_, func, **kw), r, w)

    def tt(eng, out, in0, in1, op, r, w):
        S.add(eng, lambda e: e.tensor_tensor(out, in0, in1, op), r, w)

    def ts(eng, out, in0, s1, s2, op0, op1, r, w):
        if s2 is None:
            S.add(eng, lambda e: e.tensor_scalar(out, in0, s1, None, op0), r, w)
        else:
            S.add(eng, lambda e: e.tensor_scalar(out, in0, s1, s2, op0, op1), r, w)

    def stt(eng, out, in0, scalar, in1, op0, op1, r, w):
        S.add(eng, lambda e: e.scalar_tensor_tensor(out, in0, scalar, in1, op0, op1), r, w)

    def cp(eng, out, in_, r, w):
        if eng == "act":
            S.add(eng, lambda e: e.activation(out, in_, AF.Identity), r, w)
        else:
            S.add(eng, lambda e: e.tensor_copy(out, in_), r, w)

    def recip(out, in_, r, w):
        S.add("dve", lambda e: e.reciprocal(out, in_), r, w)

    def memset(eng, v, val):
        S.add(eng, lambda e: e.memset(v.ap, val), [], [v])

    def dma(q, out, in_, r, w, sembuf, slow=False):
        if slow:
            S.add(q, lambda e: e.dma_start(out=out, in_=in_, allow_slow_non_contiguous=True), r, w, dma=True, sembuf=sembuf)
        else:
            S.add(q, lambda e: e.dma_start(out=out, in_=in_), r, w, dma=True, sembuf=sembuf)

    out_dma_bufs = []

    def dbg(name, v, ap=None):
        if not DBG:
            return
        ap = v.ap if ap is None else ap
        n = 1
        for d in ap.shape[1:]:
            n *= d
        o = dbg_off[0]
        dbg_off[0] += n
        dbg_map[name] = (o, tuple(ap.shape))
        dst = dbg_t[0:ap.shape[0], o:o + n]
        if len(ap.shape) == 3:
            dst = dst.rearrange("p (a b) -> p a b", a=ap.shape[1])
        elif len(ap.shape) == 4:
            dst = dst.rearrange("p (a b c) -> p a b c", a=ap.shape[1], b=ap.shape[2])
        b = v.bufs[0]
        if b not in out_dma_bufs:
            out_dma_bufs.append(b)
        dma("sp", dst, ap, [v], [], b)

    def dma_out(dram_ap, src_ap, srcv):
        b = srcv.bufs[0]
        if S.stopped:
            return
        if b not in out_dma_bufs:
            out_dma_bufs.append(b)
        dma("sp", dram_ap, src_ap, [srcv], [], b)

    ident_f = sb("ident_f", [128, 128], F32)
    ident_b = sb("ident_b", [128, 128], BF16)
    ones_b = sb("ones_b", [128, 128], BF16)
    zeros_f = sb("zeros_f", [128, 128], F32)
    maskpl = sb("maskpl", [128, 4], F32)
    parm16 = sb("parm16", [128, 2], F32)
    parm64 = sb("parm64", [128, 2], F32)
    nparm64 = sb("nparm64", [128, 2], F32)
    PA = sb("PA", [128, 68], F32)
    PB = sb("PB", [128, 66], F32)
    gains = V(PA.ap[:, 0:40].rearrange("p (a b) -> p a b", a=5), PA.bufs)
    g4 = V(PA.ap[:, 40:48].rearrange("p (a b) -> p a b", a=2), PA.bufs)
    bglu = V(PA.ap[:, 48:52], PA.bufs)
    Dcol = V(PA.ap[:, 52:56], PA.bufs)
    convw = V(PA.ap[:, 56:68].rearrange("p (k t) -> p t k", k=3), PA.bufs)
    fconvw = V(PB.ap.rearrange("p (k t) -> p t k", k=3), PB.bufs)
    Wz = sb("Wz", [128, 4, 8, 2, 128], BF16)
    Cj = sb("Cj", [128, 16, 8, 2, 32], BF16)
    KT = sb("KT", [128, 4, 8, 128], BF16)
    M12 = sb("M12", [128, 2, 2, 16], F32)
    a4 = sb("a4", [128, 2, 16], F32)
    R8 = sb("R8", [128, 16], F32)
    rotc = sb("rotc", [128, 16, NT // BLK], F32)
    rots = sb("rots", [128, 16, NT // BLK], F32)
    KTp = sb("KTp", [128, 8, 256], BF16)
    Vp = sb("Vp", [128, 2, 1024], BF16)
    xT = sb("xT", [128, 8, NT], F32, perk=True)
    hT = sb("hT", [128, 8, NT], BF16, perk=True)
    xst = sb_multi("xst", [128, 1024], F32, 2)
    sq = sb_multi("sq", [128, NT], BF16, 2)
    rs = sb("rs", [128, NT], F32)
    carry = sb("carry", [128, 3, 16], F32)
    cxh = sb("cxh", [128, 4, 2], F32)
    fhist_p = sb("fhist_p", [128, 22, 2], F32)
    WSLOT = 4096
    wpool_t = es.enter_context(nc.sbuf_tensor("wpool", [128, 3, WSLOT], BF16))
    wpool = [V(wpool_t[:, i], [Buf(f"wp{i}")]) for i in range(3)]
    wp_i = [0]

    NCHUNK = 48
    wsc = nc.dram_tensor("wsc", [NCHUNK, 128, WSLOT], BF16, kind="Internal").ap()
    wsc_reg = {}
    kvsc = nc.dram_tensor("kvsc", [2, BS, NMEM, D], BF16, kind="Internal").ap()
    kvsc_bufs = [Buf(f"kvsc{b}") for b in range(BS)]

    def convert_kv(b):
        for j, src in enumerate((ck, cv)):
            dma("pool", kvsc[j, b].rearrange("(m p) c -> p m c", p=128), src[b].rearrange("(m p) c -> p m c", p=128), [], [kvsc_bufs[b]], kvsc_bufs[b])
    wst_bufs = [Buf(f"wpst{i}") for i in range(3)]

    def load_w(dram2d, KTn, ncols, key=None):
        si = wp_i[0] % 3
        slot = wpool[si]
        wp_i[0] += 1
        n = KTn * ncols
        view = slot.ap[:, 0:n].rearrange("p (k c) -> p k c", k=KTn)
        if key is not None and key in wsc_reg:
            idx, scb = wsc_reg[key]
            dma("pool", slot.ap[:, 0:n], wsc[idx, :, 0:n], [scb], [slot], slot.bufs[0])
        else:
            dma("pool", view, dram2d.rearrange("(k p) c -> p k c", p=128), [], [slot], slot.bufs[0])
            if key is not None:
                idx = len(wsc_reg)
                assert idx < NCHUNK
                scb = Buf(f"wsc{idx}")
                wsc_reg[key] = (idx, scb)
                dma("sp", wsc[idx, :, 0:n], slot.ap[:, 0:n], [slot], [scb], wst_bufs[si])
        return V(view, slot.bufs)

    def norm_stage(src, KTn, Dn, gain_ap, dst, dst_k0, N, gv=None):
        pb = psn()
        for kt in range(KTn):
            sqv = sq[kt % 2]
            act(sqv.ap[:, :N], src.ap[:, kt, :N], AF.Square, [src.k(kt)], [sqv])
            mm(pb.ap[:, :N], ones_b.ap, sqv.ap[:, :N], kt == 0, kt == KTn - 1, [ones_b, sqv], [pb])
        act(rs.ap[:, :N], pb.ap[:, :N], AF.Ln, [pb], [rs], scale=1.0 / Dn, bias=EPS)
        act(rs.ap[:, :N], rs.ap[:, :N], AF.Exp, [rs], [rs], scale=-0.5)
        for kt in range(KTn):
            stt("dve", dst.ap[:, dst_k0 + kt, :N], src.ap[:, kt, :N], gain_ap[:, kt:kt + 1], rs.ap[:, :N],
                ALU.mult, ALU.mult, [src.k(kt), rs] + ([gv] if gv is not None else []), [dst.k(dst_k0 + kt)])

    def mm_stage(wdram, Kdim, cols, actv, N, evac, chunk=512, wname=None):
        KTn = Kdim // 128
        for (c0, nco) in cols:
            wv = load_w(wdram[:, c0:c0 + nco], KTn, nco, key=(wname, c0) if wname else None)
            for cti in range(nco // 128):
                pb = psn()
                for kt in range(KTn):
                    mm(pb.ap[:, :N], wv.ap[:, kt, cti * 128:(cti + 1) * 128], actv.ap[:, kt, :N],
                       kt == 0, kt == KTn - 1, [wv, actv.k(kt)], [pb])
                evac(c0 // 128 + cti, pb)

    def add_resid(N):
        def ev(ct, pb):
            tt("dve", xT.ap[:, ct, :N], pb.ap[:, :N], xT.ap[:, ct, :N], ALU.add, [pb, xT.k(ct)], [xT.k(ct)])
        return ev


    m0 = amark()
    idi = aalloc([128], I32)
    S.add("pool", lambda e: e.iota(idi.ap, pattern=[[1, 128]], base=0, channel_multiplier=-1), [], [idi])
    cp("dve", ident_f.ap, idi.ap, [idi], [ident_f])
    S.add("dve", lambda e: e.tensor_single_scalar(ident_f.ap, ident_f.ap, 0.0, ALU.is_equal), [ident_f], [ident_f])
    cp("dve", ident_b.ap, ident_f.ap, [ident_f], [ident_b])
    memset("dve", ones_b, 1.0)
    memset("dve", zeros_f, 0.0)
    memset("dve", carry, 0.0)
    memset("dve", cxh, 0.0)
    memset("dve", fhist_p, 0.0)
    AX = mybir.AxisListType.X
    S.add("dve", lambda e: e.reduce_sum(maskpl.ap, ident_f.ap.rearrange("p (a c) -> p a c", a=4), axis=AX), [ident_f], [maskpl])
    S.add("dve", lambda e: e.reduce_sum(parm64.ap, ident_f.ap.rearrange("p (a c) -> p a c", a=2), axis=AX), [ident_f], [parm64])
    tmp8 = aalloc([8], F32)
    S.add("dve", lambda e: e.reduce_sum(tmp8.ap, ident_f.ap.rearrange("p (a c) -> p a c", a=8), axis=AX), [ident_f], [tmp8])
    S.add("dve", lambda e: e.reduce_sum(parm16.ap, tmp8.ap.rearrange("p (a s) -> p s a", s=2), axis=AX), [tmp8], [parm16])
    ts("dve", nparm64.ap, parm64.ap, -1.0, None, ALU.mult, None, [parm64], [nparm64])

    stage(0)
    for s_ in range(2):
        dma("sp", xst[s_].ap, mem_p[s_ * 128:(s_ + 1) * 128, :], [], [xst[s_]], xst[s_].bufs[0])
    stA = aalloc([128], F32)
    stB = aalloc([128], F32)
    stL = aalloc([128], F32)
    stL2 = aalloc([128], F32)

    def ld_rows(st, r0, vec, nrow):
        dma("sp", st.ap[r0:r0 + nrow, :], vec.rearrange("(k p) -> k p", p=128), [], [st], st.bufs[0])

    r_ = 0
    for vec_ in (norm_mix, norm_xattn, norm_ffn, norm_final, norm_mem):
        ld_rows(stA, r_, vec_, 8)
        r_ += 8
    for vec_ in (norm_ssm_out, norm_conv_out, b_glu, D_ssm, conv_w[0], conv_w[1], conv_w[2]):
        ld_rows(stA, r_, vec_, 4)
        r_ += 4
    assert r_ == 68
    for k in range(3):
        ld_rows(stB, 22 * k, ffn_conv_w[k], 22)
    dma("sp", stL.ap[0:16, :], A_re.rearrange("(pr g) n -> pr (g n)", g=2), [], [stL], stL.bufs[0])
    dma("sp", stL2.ap[0:16, :], A_im.rearrange("(pr g) n -> pr (g n)", g=2), [], [stL2], stL2.bufs[0])
    pbA = psn()
    tr(pbA.ap[:, 0:68], stA.ap[0:68, :], ident_f.ap[0:68, 0:68], [stA, ident_f], [pbA])
    tr(pbA.ap[:, 128:194], stB.ap[0:66, :], ident_f.ap[0:66, 0:66], [stB, ident_f], [pbA])
    tr(pbA.ap[:, 256:272], stL.ap[0:16, :], ident_f.ap[0:16, 0:16], [stL, ident_f], [pbA])
    tr(pbA.ap[:, 272:288], stL2.ap[0:16, :], ident_f.ap[0:16, 0:16], [stL2, ident_f], [pbA])
    cp("act", PA.ap, pbA.ap[:, 0:68], [pbA], [PA])
    cp("act", PB.ap, pbA.ap[:, 128:194], [pbA], [PB])
    lr = aalloc([16], F32)
    li = aalloc([16], F32)
    ldt = aalloc([16], F32)
    Br = aalloc([16, 16], F32)
    Bi = aalloc([16, 16], F32)
    cp("act", lr.ap, pbA.ap[:, 256:272], [pbA], [lr])
    cp("act", li.ap, pbA.ap[:, 272:288], [pbA], [li])
    for gg in range(2):
        dma("sp", ldt.ap[64 * gg:64 * gg + 64, :], log_dt.rearrange("(pr g) -> g pr", g=2)[gg].partition_broadcast(64), [], [ldt], ldt.bufs[0], slow=True)
        dma("sp", Br.ap[64 * gg:64 * gg + 64], B_re.rearrange("(pr g) n p -> g n pr p", g=2)[gg], [], [Br], Br.bufs[0])
        dma("sp", Bi.ap[64 * gg:64 * gg + 64], B_im.rearrange("(pr g) n p -> g n pr p", g=2)[gg], [], [Bi], Bi.bufs[0])
    Cnat = aalloc([2, 4, 64], F32)
    dma("sp", Cnat.ap[:, 0], C_re.rearrange("(k g) q n -> (g q) k n", k=4), [], [Cnat], Cnat.bufs[0])
    dma("sp", Cnat.ap[:, 1], C_im.rearrange("(k g) q n -> (g q) k n", k=4), [], [Cnat], Cnat.bufs[0])
    m1 = amark()
    memT = aalloc([8, NMEM], F32)
    for s_ in range(2):
        st = xst[s_ % 2]
        for half in range(2):
            pb = psn()
            for j in range(4):
                kt = half * 4 + j
                tr(pb.ap[:, j * 128:(j + 1) * 128], st.ap[:, kt * 128:(kt + 1) * 128], ident_f.ap, [st, ident_f], [pb])
            cp("act" if half == 0 else "dve", memT.ap[:, half * 4:half * 4 + 4, s_ * 128:(s_ + 1) * 128],
               pb.ap.rearrange("p (k c) -> p k c", k=4), [pb], [memT])
    stage(2.2)
    mhT = aalloc([8, NMEM], BF16)
    norm_stage(memT, 8, D, gains.ap[:, 4, :], mhT, 0, NMEM, gv=gains)
    stage(2.4)
    kvst = [aalloc([512], F32) for _ in range(2)]
    kv_i = [0]
    for (wd, outd, isk) in ((w_k, mk_o, True), (w_v, mv_o, False)):
        for cc in range(2):
            wv = load_w(wd[:, cc * 512:(cc + 1) * 512], 8, 512)
            stage(2.5)
            if isk:
                for cti in range(4):
                    pb = psn()
                    for kt in range(8):
                        mm(pb.ap[:, :NMEM], wv.ap[:, kt, cti * 128:(cti + 1) * 128], mhT.ap[:, kt, :], kt == 0, kt == 7, [wv, mhT], [pb])
                    act(KTp.ap[:, cc * 4 + cti, :], pb.ap[:, :NMEM], AF.Identity, [pb], [KTp])
            stage(2.6)
            for mt in range(2):
                pb = psn()
                for kt in range(8):
                    mm(pb.ap, mhT.ap[:, kt, mt * 128:(mt + 1) * 128], wv.ap[:, kt, :], kt == 0, kt == 7, [wv, mhT], [pb])
                stg = kvst[kv_i[0] % 2]
                kv_i[0] += 1
                cp("act", stg.ap, pb.ap, [pb], [stg])
                if not isk:
                    act(Vp.ap[:, mt, cc * 512:(cc + 1) * 512], stg.ap, AF.Identity, [stg], [Vp])
                stage(2.65)
                dma_out(outd[mt * 128:(mt + 1) * 128, cc * 512:(cc + 1) * 512], stg.ap, stg)
                stage(2.7 + 0.01 * kv_i[0])
    stage(3)
    stage(1)
    Cnp = aalloc([2, 4, 2, 64], F32)
    for ri in range(2):
        for s_ in range(2):
            ts("dve", Cnp.ap[:, ri, :, s_, :], Cnat.ap[:, ri], parm16.ap[:, s_:s_ + 1], None, ALU.mult, None, [Cnat, parm16], [Cnp])
    Cpad = aalloc([2, 4, 128], F32)
    for ri in range(2):
        pb = psn()
        for kt in range(4):
            tr(pb.ap[:, kt * 128:(kt + 1) * 128], Cnp.ap[:, ri, kt].rearrange("p s n -> p (s n)"), ident_f.ap, [Cnp, ident_f], [pb])
        if ri == 0:
            cp("dve", Cpad.ap[:, 0].rearrange("p k c -> p (k c)"), pb.ap, [pb], [Cpad])
        else:
            ts("dve", Cpad.ap[:, 1].rearrange("p k c -> p (k c)"), pb.ap, -1.0, None, ALU.mult, None, [pb], [Cpad])

    sc = aalloc([24, 16], F32)
    def SC(i):
        return sc.ap[:, i, :]
    dtv, mag, ph, kf, r_, sn, shf, cs, ar, ai, den, rden, cr, ci, t1, t2, t3 = [SC(i) for i in range(17)]
    ki = aalloc([16], I32)
    scv = [sc]
    act(dtv, ldt.ap, AF.Exp, [ldt], scv)
    dlr = SC(17)
    tt("dve", dlr, dtv, lr.ap, ALU.mult, scv + [lr], scv)
    tt("dve", ph, dtv, li.ap, ALU.mult, scv + [li], scv)
    act(mag, dlr, AF.Exp, scv, scv)
    act(R8.ap, dlr, AF.Exp, scv, [R8], scale=8.0)
    act(SC(18), dlr, AF.Exp, scv, scv, scale=-8.0)
    ts("dve", kf, ph, 1.0 / (2 * math.pi), None, ALU.mult, None, scv, scv)
    cp("dve", ki.ap, kf, scv, [ki])
    cp("dve", kf, ki.ap, [ki], scv)
    stt("dve", r_, kf, -2 * math.pi, ph, ALU.mult, ALU.add, scv, scv)
    ts("dve", r_, r_, 3.14159, -3.14159, ALU.min, ALU.max, scv, scv)
    act(sn, r_, AF.Sin, scv, scv)
    act(shf, r_, AF.Sin, scv, scv, scale=0.5)
    tt("dve", cs, shf, shf, ALU.mult, scv, scv)
    ts("dve", cs, cs, -2.0, 1.0, ALU.mult, ALU.add, scv, scv)
    tt("dve", ar, mag, cs, ALU.mult, scv, scv)
    tt("dve", ai, mag, sn, ALU.mult, scv, scv)
    tt("dve", den, lr.ap, lr.ap, ALU.mult, scv + [lr], scv)
    tt("dve", t1, li.ap, li.ap, ALU.mult, scv + [li], scv)
    tt("dve", den, den, t1, ALU.add, scv, scv)
    recip(rden, den, scv, scv)
    ts("dve", t1, ar, -1.0, None, ALU.add, None, scv, scv)
    tt("dve", t2, t1, lr.ap, ALU.mult, scv + [lr], scv)
    tt("dve", t3, ai, li.ap, ALU.mult, scv + [li], scv)
    tt("dve", t2, t2, t3, ALU.add, scv, scv)
    tt("dve", cr, t2, rden, ALU.mult, scv, scv)
    tt("dve", t2, ai, lr.ap, ALU.mult, scv + [lr], scv)
    tt("dve", t3, t1, li.ap, ALU.mult, scv + [li], scv)
    tt("dve", t2, t2, t3, ALU.subtract, scv, scv)
    tt("dve", ci, t2, rden, ALU.mult, scv, scv)

    def bc(ap16):
        return ap16.unsqueeze(2).to_broadcast([128, 16, 16])

    def bc2(ap16, n):
        return ap16.unsqueeze(2).to_broadcast([128, 16, n])

    bb = aalloc([2, 16, 16], F32)
    tb = aalloc([16, 16], F32)
    tt("dve", bb.ap[:, 0], Br.ap, bc(cr), ALU.mult, [Br] + scv, [bb])
    tt("dve", tb.ap, Bi.ap, bc(ci), ALU.mult, [Bi] + scv, [tb])
    tt("dve", bb.ap[:, 0], bb.ap[:, 0], tb.ap, ALU.subtract, [bb, tb], [bb])
    tt("dve", bb.ap[:, 1], Bi.ap, bc(cr), ALU.mult, [Bi] + scv, [bb])
    tt("dve", tb.ap, Br.ap, bc(ci), ALU.mult, [Br] + scv, [tb])
    tt("dve", bb.ap[:, 1], bb.ap[:, 1], tb.ap, ALU.add, [bb, tb], [bb])
    pw = aalloc([9, 2, 16], F32)
    S.add("dve", lambda e: e.memset(pw.ap[:, 0, 0, :], 1.0), [], [pw])
    S.add("dve", lambda e: e.memset(pw.ap[:, 0, 1, :], 0.0), [], [pw])
    cp("dve", pw.ap[:, 1, 0, :], ar, scv, [pw])
    cp("dve", pw.ap[:, 1, 1, :], ai, scv, [pw])
    for k in range(2, 9):
        tt("dve", pw.ap[:, k, 0, :], pw.ap[:, k - 1, 0, :], ar, ALU.mult, [pw] + scv, [pw])
        tt("dve", t1, pw.ap[:, k - 1, 1, :], ai, ALU.mult, [pw] + scv, scv)
        tt("dve", pw.ap[:, k, 0, :], pw.ap[:, k, 0, :], t1, ALU.subtract, [pw] + scv, [pw])
        tt("dve", pw.ap[:, k, 1, :], pw.ap[:, k - 1, 0, :], ai, ALU.mult, [pw] + scv, [pw])
        tt("dve", t1, pw.ap[:, k - 1, 1, :], ar, ALU.mult, [pw] + scv, scv)
        tt("dve", pw.ap[:, k, 1, :], pw.ap[:, k, 1, :], t1, ALU.add, [pw] + scv, [pw])
    cp("dve", M12.ap[:, 0, 0, :], pw.ap[:, 8, 0, :], [pw], [M12])
    cp("dve", M12.ap[:, 0, 1, :], pw.ap[:, 8, 0, :], [pw], [M12])
    ts("dve", M12.ap[:, 1, 0, :], pw.ap[:, 8, 1, :], -1.0, None, ALU.mult, None, [pw], [M12])
    cp("dve", M12.ap[:, 1, 1, :], pw.ap[:, 8, 1, :], [pw], [M12])
    cp("dve", a4.ap, pw.ap[:, 4], [pw], [a4])
    NBr = NT // BLK
    cur_r, cur_i, nx_r, nx_i, tq1 = SC(19), SC(20), SC(21), SC(22), SC(23)
    tt("pool", cur_r, pw.ap[:, 8, 0, :], SC(18), ALU.mult, [pw] + scv, scv)
    tt("pool", cur_i, pw.ap[:, 8, 1, :], SC(18), ALU.mult, [pw] + scv, scv)
    cp("pool", rotc.ap[:, :, 0], cur_r, scv, [rotc])
    cp("pool", rots.ap[:, :, 0], cur_i, scv, [rots])
    tBr = aalloc([16, NBr // 2], F32)
    m_ = 1
    while m_ < NBr:
        br_ = cur_r.unsqueeze(2).to_broadcast([128, 16, m_])
        bi_ = cur_i.unsqueeze(2).to_broadcast([128, 16, m_])
        tbv = tBr.ap[:, :, 0:m_]
        tt("pool", rotc.ap[:, :, m_:2 * m_], rotc.ap[:, :, 0:m_], br_, ALU.mult, [rotc] + scv, [rotc])
        tt("pool", tbv, rots.ap[:, :, 0:m_], bi_, ALU.mult, [rots] + scv, [tBr])
        tt("pool", rotc.ap[:, :, m_:2 * m_], rotc.ap[:, :, m_:2 * m_], tbv, ALU.subtract, [rotc, tBr], [rotc])
        tt("pool", rots.ap[:, :, m_:2 * m_], rotc.ap[:, :, 0:m_], bi_, ALU.mult, [rotc] + scv, [rots])
        tt("pool", tbv, rots.ap[:, :, 0:m_], br_, ALU.mult, [rots] + scv, [tBr])
        tt("pool", rots.ap[:, :, m_:2 * m_], rots.ap[:, :, m_:2 * m_], tbv, ALU.add, [rots, tBr], [rots])
        m_ *= 2
        if m_ < NBr:
            tt("pool", nx_r, cur_r, cur_r, ALU.mult, scv, scv)
            tt("pool", tq1, cur_i, cur_i, ALU.mult, scv, scv)
            tt("pool", nx_r, nx_r, tq1, ALU.subtract, scv, scv)
            tt("pool", nx_i, cur_r, cur_i, ALU.mult, scv, scv)
            ts("pool", nx_i, nx_i, 2.0, None, ALU.mult, None, scv, scv)
            cp("pool", cur_r, nx_r, scv, scv)
            cp("pool", cur_i, nx_i, scv, scv)

    CpR = Cpad.ap[:, 0].rearrange("p k (a c) -> p (k a) c", a=4)
    CpNI = Cpad.ap[:, 1].rearrange("p k (a c) -> p (k a) c", a=4)
    tf = aalloc([2, 16, 32], F32)
    for k in range(1, 9):
        pr_ = bc2(pw.ap[:, k, 0, :], 32)
        pi_ = bc2(pw.ap[:, k, 1, :], 32)
        tt("pool", tf.ap[:, 0], CpR, pr_, ALU.mult, [Cpad, pw], [tf])
        tt("pool", tf.ap[:, 1], CpNI, pi_, ALU.mult, [Cpad, pw], [tf])
        tt("pool", Cj.ap[:, :, k - 1, 0, :], tf.ap[:, 0], tf.ap[:, 1], ALU.add, [tf], [Cj])
        tt("pool", tf.ap[:, 0], CpNI, pr_, ALU.mult, [Cpad, pw], [tf])
        tt("pool", tf.ap[:, 1], CpR, pi_, ALU.mult, [Cpad, pw], [tf])
        tt("pool", Cj.ap[:, :, k - 1, 1, :], tf.ap[:, 0], tf.ap[:, 1], ALU.subtract, [tf], [Cj])

    Ek = aalloc([2, 16, 16], F32)
    Epads = [aalloc([2, 16, 2, 16], F32) for _ in range(2)]
    diagD = aalloc([4, 128], F32)
    for kt in range(4):
        ts("dve", diagD.ap[:, kt, :], ident_f.ap, Dcol.ap[:, kt:kt + 1], None, ALU.mult, None, [ident_f, Dcol], [diagD])
    for k in range(8):
        Ep = Epads[k % 2]
        pr_ = bc(pw.ap[:, k, 0, :])
        pi_ = bc(pw.ap[:, k, 1, :])
        tt("dve", Ek.ap[:, 0], bb.ap[:, 0], pr_, ALU.mult, [bb, pw], [Ek])
        tt("dve", tb.ap, bb.ap[:, 1], pi_, ALU.mult, [bb, pw], [tb])
        tt("dve", Ek.ap[:, 0], Ek.ap[:, 0], tb.ap, ALU.subtract, [Ek, tb], [Ek])
        tt("dve", Ek.ap[:, 1], bb.ap[:, 1], pr_, ALU.mult, [bb, pw], [Ek])
        tt("dve", tb.ap, bb.ap[:, 0], pi_, ALU.mult, [bb, pw], [tb])
        tt("dve", Ek.ap[:, 1], Ek.ap[:, 1], tb.ap, ALU.add, [Ek, tb], [Ek])
        for ri in range(2):
            for s_ in range(2):
                ts("dve", Ep.ap[:, ri, :, s_, :], Ek.ap[:, ri], parm64.ap[:, s_:s_ + 1], None, ALU.mult, None, [Ek, parm64], [Ep])
        for ri in range(2):
            pb = psn()
            for kt in range(4):
                tr(pb.ap[:, kt * 128:(kt + 1) * 128],
                   Ep.ap[:, ri, 4 * kt:4 * kt + 4].rearrange("p a s c -> p (a s c)"), ident_f.ap, [Ep, ident_f], [pb])
            cp("act", Wz.ap[:, :, 7 - k, ri, :], pb.ap.rearrange("p (k c) -> p k c", k=4), [pb], [Wz])
        pb = psn()
        for z4 in range(4):
            mm(pb.ap[:, z4 * 128:(z4 + 1) * 128], zeros_f.ap, zeros_f.ap, True, False, [zeros_f], [pb])
        for kt in range(4):
            for pl in range(4):
                pair = 4 * kt + pl
                for ri in range(2):
                    last = (kt == 3 and pl == 3 and ri == 1)
                    mm(pb.ap[32 * pl:32 * pl + 32, kt * 128 + 32 * pl:kt * 128 + 32 * pl + 32],
                       Ep.ap[:, ri, pair].rearrange("p s c -> p (s c)"),
                       Cpad.ap[:, ri, kt, 32 * pl:32 * pl + 32], False, last, [Ep, Cpad], [pb], tp=(0, 32 * pl))
        if k == 0:
            tt("dve", KT.ap[:, :, 0, :], pb.ap.rearrange("p (k c) -> p k c", k=4), diagD.ap, ALU.add, [pb, diagD], [KT])
        else:
            cp("act", KT.ap[:, :, k, :], pb.ap.rearrange("p (k c) -> p k c", k=4), [pb], [KT])
    dbg("sc", sc)
    dbg("pw", pw)
    dbg("bb", bb)
    dbg("lr", lr)
    dbg("li", li)
    dbg("ldt", ldt)
    dbg("a4", a4)
    dbg("M12", M12)
    dbg("Cpad", Cpad)
    dbg("parm64", parm64)
    dbg("parm16", parm16)
    dbg("maskpl", maskpl)
    build_program.dbg_map = dbg_map
    arelease(m0)
    stage(2)

    xstage = aalloc([4, 1024], F32, at=72)

    def prefetch_x(ti):
        for s_ in range(4):
            v = xstage.k(s_)
            dma("sp", v.ap, x_p[ti * NT + s_ * 128: ti * NT + (s_ + 1) * 128, :], [], [v], v.bufs[0])

    def run_tile(kind, ti):
        prompt = kind == "p"
        N = NT if prompt else NS
        nb, T = (1, NT) if prompt else (BS, TS)
        last_p = prompt and ti == SEQ // NT - 1
        mt0_ = amark()
        if LIMIT > 3:
            S.stopped = False
        NKV = 6
        kvbufs = []
        if not prompt:
            for j in range(NKV):
                kr = aalloc([2, 1024], BF16)
                vb_ = aalloc([2, 1024], BF16)
                kvbufs.append((kr, vb_))
                dma("pool", kr.ap, kvsc[0, j].rearrange("(m p) c -> p m c", p=128), [kvsc_bufs[j]], [kr], kr.bufs[0])
                dma("pool", vb_.ap, kvsc[1, j].rearrange("(m p) c -> p m c", p=128), [kvsc_bufs[j]], [vb_], vb_.bufs[0])
        mt_ = amark()

        if prompt:
            if ti == 0:
                prefetch_x(0)
            for s_ in range(4):
                st = xstage.k(s_)
                for half in range(2):
                    pb = psn()
                    for j in range(4):
                        kt = half * 4 + j
                        tr(pb.ap[:, j * 128:(j + 1) * 128], st.ap[:, kt * 128:(kt + 1) * 128], ident_f.ap, [st, ident_f], [pb])
                    cp("act" if half == 0 else "dve", xT.ap[:, half * 4:half * 4 + 4, s_ * 128:(s_ + 1) * 128],
                       pb.ap.rearrange("p (k c) -> p k c", k=4), [pb], [xT.ks(half * 4, half * 4 + 4)])
        else:
            st = xst[0]
            dma("sp", st.ap[0:NS, :], x_s, [], [st], st.bufs[0])
            for half in range(2):
                pb = psn()
                for j in range(4):
                    kt = half * 4 + j
                    tr(pb.ap[:, j * NS:(j + 1) * NS], st.ap[0:NS, kt * 128:(kt + 1) * 128], ident_f.ap[0:NS, 0:NS], [st, ident_f], [pb])
                cp("act" if half == 0 else "dve", xT.ap[:, half * 4:half * 4 + 4, 0:NS],
                   pb.ap[:, 0:4 * NS].rearrange("p (k c) -> p k c", k=4), [pb], [xT.ks(half * 4, half * 4 + 4)])

        stage(4)
        norm_stage(xT, 8, D, gains.ap[:, 0, :], hT, 0, N, gv=gains)
        umask = aalloc([4, 4, N], BF16)
        ufull = aalloc([4, N], BF16)
        cx_p0 = amark()
        cx = aalloc([4, nb, T + 2], F32)
        cvb_p0 = amark()
        cvb = aalloc([4, nb, T], F32)
        mixedT = aalloc([8, N], BF16)
        if prompt:
            cp("dve", cx.ap[:, :, 0, 0:2], cxh.ap, [cxh], [cx])
        else:
            scs = aalloc([512], F32)
            dma("sp", scs.ap[0:32, :], st_conv, [], [scs], scs.bufs[0])
            pb = psn()
            for kt in range(4):
                tr(pb.ap[:, kt * 32:(kt + 1) * 32], scs.ap[0:32, kt * 128:(kt + 1) * 128], ident_f.ap[0:32, 0:32], [scs, ident_f], [pb])
            cp("dve", cx.ap[:, :, :, 0:2], pb.ap[:, 0:128].rearrange("p (k b c) -> p k b c", k=4, b=BS), [pb], [cx])

        def ev_win(ct, pb):
            if ct < 4:
                cp("dve", ufull.ap[:, ct, :], pb.ap[:, :N], [pb], [ufull.k(ct)])
                for pl in range(4):
                    if pl < 2:
                        act(umask.ap[:, ct, pl, :], ufull.ap[:, ct, :], AF.Identity, [ufull.k(ct), maskpl], [umask.k(ct)], scale=maskpl.ap[:, pl:pl + 1])
                    else:
                        ts("dve", umask.ap[:, ct, pl, :], ufull.ap[:, ct, :], maskpl.ap[:, pl:pl + 1], None, ALU.mult, None, [ufull.k(ct), maskpl], [umask.k(ct)])
            elif ct < 8:
                act(cx.ap[:, ct - 4, :, 2:2 + T], pb.ap[:, :N].rearrange("p (b t) -> p b t", b=nb), AF.Identity, [pb], [cx])
            elif ct >= 12:
                kt = ct - 12
                tt("dve", cx.ap[:, kt, :, 2:2 + T], pb.ap[:, :N].rearrange("p (b t) -> p b t", b=nb),
                   cx.ap[:, kt, :, 2:2 + T], ALU.mult, [pb, cx], [cx])
                ts("dve", cvb.ap[:, kt], cx.ap[:, kt, :, 2:2 + T], convw.ap[:, kt, 2:3], None, ALU.mult, None, [cx, convw], [cvb])
                stt("dve", cvb.ap[:, kt], cx.ap[:, kt, :, 1:1 + T], convw.ap[:, kt, 1:2], cvb.ap[:, kt], ALU.mult, ALU.add, [cx, convw, cvb], [cvb])
                stt("dve", cvb.ap[:, kt], cx.ap[:, kt, :, 0:T], convw.ap[:, kt, 0:1], cvb.ap[:, kt], ALU.mult, ALU.add, [cx, convw, cvb], [cvb])
            else:
                kt = ct - 8
                tt("dve", cvb.ap[:, kt], pb.ap[:, :N].rearrange("p (b t) -> p b t", b=nb), cvb.ap[:, kt], ALU.mult, [pb, cvb], [cvb])

        ssmA = {}

        def ssm_part_a():
            NB = NT // BLK
            Sbf = aalloc([2, 16, NB], BF16)
            Wt = aalloc([2, 16, NB], F32)
            Vt = aalloc([2, 16, NB], F32)
            T12 = aalloc([2, 16, NB], F32)
            zb = [[psn(), psn()] for _ in range(2)]
            for pair in range(16):
                kt, pl = pair // 4, pair % 4
                for ri in range(2):
                    zv = zb[ri][pair // 8]
                    for i in range(BLK):
                        mm(zv.ap[:, (pair % 8) * NB:(pair % 8 + 1) * NB], Wz.ap[:, kt, i, ri, :],
                           umask.ap[:, kt, pl, :].rearrange("p (c j) -> p c j", j=BLK)[:, :, i],
                           i == 0, i == BLK - 1, [Wz, umask.k(kt)], [zv])
            for hf in range(2):
                zr = zb[0][hf].ap.rearrange("p (a c) -> p a c", a=8)
                zi = zb[1][hf].ap.rearrange("p (a c) -> p a c", a=8)
                c_ = rotc.ap[:, hf * 8:(hf + 1) * 8, :]
                s_ = rots.ap[:, hf * 8:(hf + 1) * 8, :]
                t1 = T12.ap[:, 0, hf * 8:(hf + 1) * 8, :]
                t2 = T12.ap[:, 1, hf * 8:(hf + 1) * 8, :]
                wr = Wt.ap[:, 0, hf * 8:(hf + 1) * 8, :]
                wi = Wt.ap[:, 1, hf * 8:(hf + 1) * 8, :]
                tt("dve", t1, zr, c_, ALU.mult, [zb[0][hf], rotc], [T12])
                tt("dve", t2, zi, s_, ALU.mult, [zb[1][hf], rots], [T12])
                tt("dve", wr, t1, t2, ALU.add, [T12], [Wt])
                tt("dve", t1, zi, c_, ALU.mult, [zb[1][hf], rotc], [T12])
                tt("dve", t2, zr, s_, ALU.mult, [zb[0][hf], rots], [T12])
                tt("dve", wi, t1, t2, ALU.subtract, [T12], [Wt])
            for ri in range(2):
                for pair in range(16):
                    S.add("dve", lambda e, ri=ri, pair=pair: e.tensor_tensor_scan(
                        Vt.ap[:, ri, pair, :], R8.ap[:, pair:pair + 1].to_broadcast([128, NB]), Wt.ap[:, ri, pair, :],
                        carry.ap[:, ri, pair:pair + 1], ALU.mult, ALU.add), [R8, Wt, carry], [Vt])
            cp("dve", Sbf.ap[:, :, :, 0], carry.ap[:, 0:2, :], [carry], [Sbf])
            t1 = T12.ap[:, 0]
            t2 = T12.ap[:, 1]
            tt("dve", t1, rotc.ap, Vt.ap[:, 0], ALU.mult, [rotc, Vt], [T12])
            tt("dve", t2, rots.ap, Vt.ap[:, 1], ALU.mult, [rots, Vt], [T12])
            tt("dve", Sbf.ap[:, 0, :, 1:NB], t1[:, :, 0:NB - 1], t2[:, :, 0:NB - 1], ALU.subtract, [T12], [Sbf])
            tt("dve", carry.ap[:, 0, :], t1[:, :, NB - 1], t2[:, :, NB - 1], ALU.subtract, [T12], [carry])
            tt("dve", t1, rotc.ap, Vt.ap[:, 1], ALU.mult, [rotc, Vt], [T12])
            tt("dve", t2, rots.ap, Vt.ap[:, 0], ALU.mult, [rots, Vt], [T12])
            tt("dve", Sbf.ap[:, 1, :, 1:NB], t1[:, :, 0:NB - 1], t2[:, :, 0:NB - 1], ALU.add, [T12], [Sbf])
            tt("dve", carry.ap[:, 1, :], t1[:, :, NB - 1], t2[:, :, NB - 1], ALU.add, [T12], [carry])
            ssmA["Sbf"] = Sbf

        mm_stage(w_in, D, [(0, 512), (512, 512), (1536, 512)], hT, N, ev_win, wname="w_in")
        if prompt:
            ssm_part_a()
        mm_stage(w_in, D, [(1024, 512)], hT, N, ev_win, wname="w_in")

        if prompt:
            cp("dve", cxh.ap, cx.ap[:, :, 0, T:T + 2], [cx], [cxh])
            if last_p:
                pb = psn()
                for kt in range(4):
                    tr(pb.ap[0:2, kt * 128:(kt + 1) * 128], cxh.ap[:, kt, :], ident_f.ap, [cxh, ident_f], [pb])
                ostg = aalloc([512], F32)
                cp("dve", ostg.ap[0:2, :], pb.ap[0:2, :], [pb], [ostg])
                dma_out(cb_p, ostg.ap[0:2, :], ostg)
        else:
            cl = aalloc([4, BS, 2], F32)
            cp("dve", cl.ap, cx.ap[:, :, :, T:T + 2], [cx], [cl])
            pb = psn()
            for kt in range(4):
                tr(pb.ap[0:32, kt * 128:(kt + 1) * 128], cl.ap[:, kt].rearrange("p b c -> p (b c)"), ident_f.ap, [cl, ident_f], [pb])
            ostg = aalloc([512], F32)
            cp("dve", ostg.ap[0:32, :], pb.ap[0:32, :], [pb], [ostg])
            dma_out(cb_s, ostg.ap[0:32, :], ostg)
        cvf = V(cvb.ap.rearrange("p k b t -> p k (b t)"), cvb.bufs)
        norm_stage(cvf, 4, 512, g4.ap[:, 1, :], mixedT, 4, N, gv=g4)

        stage(5)
        g1 = aalloc([4, N], F32, at=cx_p0)
        g1b = aalloc([4, N], BF16)
        if prompt:
            NB = NT // BLK
            Sbf = ssmA["Sbf"]
            stage(5.6)
            for kt in range(4):
                yb = psn()
                y3 = yb.ap.rearrange("p (c j) -> p c j", j=BLK)
                u3 = ufull.ap[:, kt, :].rearrange("p (c j) -> p c j", j=BLK)
                for tau in range(BLK):
                    mm(y3[:, :, tau:BLK], KT.ap[:, kt, tau, :], u3[:, :, 0:BLK - tau], tau == 0, False, [KT, ufull.k(kt)], [yb])
                for pl in range(4):
                    pair = 4 * kt + pl
                    for ri in range(2):
                        for j in range(BLK):
                            last = (pl == 3 and ri == 1 and j == BLK - 1)
                            mm(y3[32 * pl:32 * pl + 32, :, j], Cj.ap[:, pair, j, ri, :], Sbf.ap[:, ri, pair, :],
                               False, last, [Cj, Sbf], [yb], tp=(0, 32 * pl))
                act(g1.ap[:, kt, :], yb.ap, AF.Gelu_apprx_tanh, [yb], [g1.k(kt)])
                cp("dve", g1b.ap[:, kt, :], g1.ap[:, kt, :], [g1.k(kt)], [g1b.k(kt)])
            stage(5.8)
            if last_p:
                pb = psn()
                for ri in range(2):
                    tr(pb.ap[0:16, ri * 128:(ri + 1) * 128], carry.ap[:, ri, :], ident_f.ap, [carry, ident_f], [pb])
                sst = aalloc([256], F32)
                cp("dve", sst.ap[0:16, :], pb.ap[0:16, 0:256], [pb], [sst])
                dma_out(sre_p, sst.ap[0:16, 0:128], sst)
                dma_out(sim_p, sst.ap[0:16, 128:256], sst)
        else:
            sstg = aalloc([2, 2048], F32)
            dma("sp", sstg.ap[0:BS, 0, :], st_re, [], [sstg], sstg.bufs[0])
            dma("sp", sstg.ap[0:BS, 1, :], st_im, [], [sstg], sstg.bufs[0])
            S0 = aalloc([2, 16, BS], F32)
            S0b = aalloc([2, 16, BS], BF16)
            for ri in range(2):
                pb = psn()
                for pair in range(16):
                    tr(pb.ap[:, pair * BS:(pair + 1) * BS], sstg.ap[0:BS, ri, pair * 128:(pair + 1) * 128],
                       ident_f.ap[0:BS, 0:BS], [sstg, ident_f], [pb])
                cp("dve", S0.ap[:, ri].rearrange("p a b -> p (a b)"), pb.ap[:, 0:256], [pb], [S0])
                act(S0b.ap[:, ri].rearrange("p a b -> p (a b)"), S0.ap[:, ri].rearrange("p a b -> p (a b)"), AF.Identity, [S0], [S0b])
            for kt in range(4):
                yb = psn()
                y3 = yb.ap[:, :N].rearrange("p (b t) -> p b t", t=TS)
                u3 = ufull.ap[:, kt, :].rearrange("p (b t) -> p b t", t=TS)
                for tau in range(TS):
                    mm(y3[:, :, tau:TS], KT.ap[:, kt, tau, :], u3[:, :, 0:TS - tau], tau == 0, False, [KT, ufull.k(kt)], [yb])
                for pl in range(4):
                    pair = 4 * kt + pl
                    for ri in range(2):
                        for j in range(TS):
                            last = (pl == 3 and ri == 1 and j == TS - 1)
                            mm(y3[32 * pl:32 * pl + 32, :, j], Cj.ap[:, pair, j, ri, :], S0b.ap[:, ri, pair, :],
                               False, last, [Cj, S0b], [yb], tp=(0, 32 * pl))
                act(g1.ap[:, kt, :], yb.ap[:, :N], AF.Gelu_apprx_tanh, [yb], [g1.k(kt)])
                cp("dve", g1b.ap[:, kt, :], g1.ap[:, kt, :], [g1.k(kt)], [g1b.k(kt)])
            zb = [psn(), psn()]
            for pair in range(16):
                kt, pl = pair // 4, pair % 4
                for ri in range(2):
                    for i in range(TS):
                        mm(zb[ri].ap[:, pair * BS:(pair + 1) * BS], Wz.ap[:, kt, i + 4, ri, :],
                           umask.ap[:, kt, pl, :].rearrange("p (b t) -> p b t", t=TS)[:, :, i],
                           i == 0, i == TS - 1, [Wz, umask.k(kt)], [zb[ri]])
            S3 = aalloc([2, 16, BS], F32)
            tq = aalloc([16, BS], F32)
            a4r = a4.ap[:, 0, :].unsqueeze(2).to_broadcast([128, 16, BS])
            a4i = a4.ap[:, 1, :].unsqueeze(2).to_broadcast([128, 16, BS])
            z0 = zb[0].ap[:, 0:256].rearrange("p (a b) -> p a b", a=16)
            z1 = zb[1].ap[:, 0:256].rearrange("p (a b) -> p a b", a=16)
            tt("dve", tq.ap, S0.ap[:, 0], a4r, ALU.mult, [S0, a4], [tq])
            tt("dve", S3.ap[:, 0], z0, tq.ap, ALU.add, [zb[0], tq], [S3])
            tt("dve", tq.ap, S0.ap[:, 1], a4i, ALU.mult, [S0, a4], [tq])
            tt("dve", S3.ap[:, 0], S3.ap[:, 0], tq.ap, ALU.subtract, [S3, tq], [S3])
            tt("dve", tq.ap, S0.ap[:, 1], a4r, ALU.mult, [S0, a4], [tq])
            tt("dve", S3.ap[:, 1], z1, tq.ap, ALU.add, [zb[1], tq], [S3])
            tt("dve", tq.ap, S0.ap[:, 0], a4i, ALU.mult, [S0, a4], [tq])
            tt("dve", S3.ap[:, 1], S3.ap[:, 1], tq.ap, ALU.add, [S3, tq], [S3])
            for ri in range(2):
                for q4 in range(4):
                    pb = psn()
                    for j in range(4):
                        pair = q4 * 4 + j
                        tr(pb.ap[0:BS, j * 128:(j + 1) * 128], S3.ap[:, ri, pair, :], ident_f.ap, [S3, ident_f], [pb])
                    cp("act" if q4 % 2 == 0 else "dve", sstg.ap[0:BS, ri, q4 * 512:(q4 + 1) * 512], pb.ap[0:BS, :], [pb], [sstg])
            dma_out(sre_s, sstg.ap[0:BS, 0, :], sstg)
            dma_out(sim_s, sstg.ap[0:BS, 1, :], sstg)

        stage(6)
        sg = aalloc([4, N], F32, at=cvb_p0)

        def ev_glu(ct, pb):
            act(sg.ap[:, ct, :], pb.ap[:, :N], AF.Sigmoid, [pb, bglu], [sg.k(ct)], bias=bglu.ap[:, ct:ct + 1])
            tt("dve", sg.ap[:, ct, :], sg.ap[:, ct, :], g1.ap[:, ct, :], ALU.mult, [sg.k(ct), g1.k(ct)], [sg.k(ct)])

        mm_stage(w_glu, 512, [(0, 512)], g1b, N, ev_glu, wname="w_glu")
        norm_stage(sg, 4, 512, g4.ap[:, 0, :], mixedT, 0, N, gv=g4)
        mm_stage(w_out, D, [(0, 512), (512, 512)], mixedT, N, add_resid(N), wname="w_out")
        arelease(mt_)

        stage(7)
        norm_stage(xT, 8, D, gains.ap[:, 1, :], hT, 0, N, gv=gains)
        qT = aalloc([8, N], BF16)
        oT = aalloc([8, N], BF16)

        def ev_q(ct, pb):
            act(qT.ap[:, ct, :], pb.ap[:, :N], AF.Identity, [pb], [qT.k(ct)], scale=1.0 / 16.0)

        mm_stage(w_q, D, [(0, 512), (512, 512)], hT, N, ev_q, wname="w_q")
        if prompt:
            PT = [aalloc([2, N], BF16) for _ in range(2)]
            rsa = [aalloc([N], F32) for _ in range(2)]
            for h in range(4):
                Pv = PT[h % 2]
                for mt in range(2):
                    pb = psn()
                    for dt_ in range(2):
                        mm(pb.ap, KTp.ap[:, 2 * h + dt_, mt * 128:(mt + 1) * 128], qT.ap[:, 2 * h + dt_, :],
                           dt_ == 0, dt_ == 1, [KTp, qT.k(2 * h + dt_)], [pb])
                    act(Pv.ap[:, mt, :], pb.ap, AF.Exp, [pb], [Pv])
                pb = psn()
                for mt in range(2):
                    mm(pb.ap, ones_b.ap, Pv.ap[:, mt, :], mt == 0, mt == 1, [ones_b, Pv], [pb])
                rv = rsa[h % 2]
                act(rv.ap, pb.ap, AF.Ln, [pb], [rv])
                act(rv.ap, rv.ap, AF.Exp, [rv], [rv], scale=-1.0)
                for dt_ in range(2):
                    pb = psn()
                    for mt in range(2):
                        mm(pb.ap, Vp.ap[:, mt, h * 256 + dt_ * 128:h * 256 + (dt_ + 1) * 128], Pv.ap[:, mt, :],
                           mt == 0, mt == 1, [Vp, Pv], [pb])
                    tt("dve", oT.ap[:, 2 * h + dt_, :], pb.ap, rv.ap, ALU.mult, [pb, rv], [oT.k(2 * h + dt_)])
        else:
            KTb = [aalloc([8, NMEM], BF16) for _ in range(2)]
            PTb = [aalloc([8, TS], BF16) for _ in range(2)]
            ob = psn()
            sm = psn()
            ps_avoid.extend([ob, sm])
            def s_load_tr(b):
                kr, vb_ = kvbufs[b % NKV]
                ktb = KTb[b % 2]
                if b >= NKV:
                    dma("pool", kr.ap, kvsc[0, b].rearrange("(m p) c -> p m c", p=128), [kvsc_bufs[b]], [kr], kr.bufs[0])
                    dma("pool", vb_.ap, kvsc[1, b].rearrange("(m p) c -> p m c", p=128), [kvsc_bufs[b]], [vb_], vb_.bufs[0])
                for mt in range(2):
                    pb = psn()
                    pbb = pb.ap.bitcast(BF16)
                    for hd in range(8):
                        tr(pbb[:, hd * 128:(hd + 1) * 128], kr.ap[:, mt, hd * 128:(hd + 1) * 128], ident_b.ap, [kr, ident_b], [pb])
                    cp("act" if mt == 0 else "dve", ktb.ap[:, :, mt * 128:(mt + 1) * 128],
                       pbb.rearrange("p (k c) -> p k c", k=8), [pb], [ktb])

            def s_scores(b):
                ktb, ptb = KTb[b % 2], PTb[b % 2]
                pb = psn()
                for h in range(4):
                    for mt in range(2):
                        for dt_ in range(2):
                            mm(pb.ap[:, (2 * h + mt) * TS:(2 * h + mt + 1) * TS], ktb.ap[:, 2 * h + dt_, mt * 128:(mt + 1) * 128],
                               qT.ap[:, 2 * h + dt_, b * TS:(b + 1) * TS], dt_ == 0, dt_ == 1, [ktb, qT], [pb])
                act(ptb.ap.rearrange("p a t -> p (a t)"), pb.ap[:, 0:8 * TS], AF.Exp, [pb], [ptb])

            def s_pv(b):
                kr, vb_ = kvbufs[b % NKV]
                ptb = PTb[b % 2]
                for h in range(4):
                    for dt_ in range(2):
                        for mt in range(2):
                            mm(ob.ap[:, (2 * h + dt_) * NS + b * TS:(2 * h + dt_) * NS + (b + 1) * TS],
                               vb_.ap[:, mt, h * 256 + dt_ * 128:h * 256 + (dt_ + 1) * 128], ptb.ap[:, 2 * h + mt, :],
                               mt == 0, mt == 1, [vb_, ptb], [ob])
                    for mt in range(2):
                        mm(sm.ap[:, h * NS + b * TS:h * NS + (b + 1) * TS], ones_b.ap, ptb.ap[:, 2 * h + mt, :],
                           mt == 0, mt == 1, [ones_b, ptb], [sm])

            s_load_tr(0)
            for b in range(BS):
                s_scores(b)
                if b + 1 < BS:
                    s_load_tr(b + 1)
                s_pv(b)
            del ps_avoid[:]
            rs4 = aalloc([4, NS], F32)
            recip(rs4.ap.rearrange("p a b -> p (a b)"), sm.ap[:, 0:4 * NS], [sm], [rs4])
            for dt_ in range(2):
                tt("dve", oT.ap.rearrange("p (h d) n -> p h d n", d=2)[:, :, dt_, :],
                   ob.ap.rearrange("p (h d n) -> p h d n", h=4, d=2)[:, :, dt_, :], rs4.ap, ALU.mult, [ob, rs4], [oT])
        mm_stage(w_xo, D, [(0, 512), (512, 512)], oT, N, add_resid(N), wname="w_xo")
        arelease(mt_)

        stage(8)
        if prompt and not last_p:
            prefetch_x(ti + 1)
        if prompt and ti >= 1:
            for b_ in {1: range(0, 5), 2: range(5, 10), 3: range(10, 16)}[ti]:
                convert_kv(b_)
        norm_stage(xT, 8, D, gains.ap[:, 2, :], hT, 0, N, gv=gains)
        ffT = aalloc([22, N], BF16)
        upc = [aalloc([nb, T + 2], F32) for _ in range(2)]
        av = [aalloc([nb, T], F32) for _ in range(2)]
        if prompt:
            fh = V(fhist_p.ap.rearrange("p k (b c) -> p k b c", b=1), fhist_p.bufs)
        else:
            fh = aalloc([22, BS, 2], F32)
            fstg = aalloc([DFF], F32)
            dma("sp", fstg.ap[0:32, :], st_ffn, [], [fstg], fstg.bufs[0])
            for q6 in range(6):
                pb = psn()
                nct = min(4, 22 - q6 * 4)
                for j in range(nct):
                    ct = q6 * 4 + j
                    tr(pb.ap[:, j * 32:(j + 1) * 32], fstg.ap[0:32, ct * 128:(ct + 1) * 128], ident_f.ap[0:32, 0:32], [fstg, ident_f], [pb])
                cp("dve", fh.ap[:, q6 * 4:q6 * 4 + nct].rearrange("p k b c -> p k (b c)"),
                   pb.ap[:, 0:nct * 32].rearrange("p (k c) -> p k c", k=nct), [pb], [fh])
        up_i = [0]
        ffn_cols = [(c0, min(512, DFF - c0)) for c0 in range(0, DFF, 512)]
        for (c0, nco) in ffn_cols:
            wu = load_w(w_up[:, c0:c0 + nco], 8, nco, key=("w_up", c0))
            wg = load_w(w_gate[:, c0:c0 + nco], 8, nco, key=("w_gate", c0))
            for cti in range(nco // 128):
                ct = c0 // 128 + cti
                pu = psn()
                for kt in range(8):
                    mm(pu.ap[:, :N], wu.ap[:, kt, cti * 128:(cti + 1) * 128], hT.ap[:, kt, :N], kt == 0, kt == 7, [wu, hT.k(kt)], [pu])
                pg = psn()
                for kt in range(8):
                    mm(pg.ap[:, :N], wg.ap[:, kt, cti * 128:(cti + 1) * 128], hT.ap[:, kt, :N], kt == 0, kt == 7, [wg, hT.k(kt)], [pg])
                uc = upc[up_i[0] % 2]
                a_ = av[up_i[0] % 2]
                up_i[0] += 1
                cp("dve", uc.ap[:, :, 0:2], fh.ap[:, ct], [fh], [uc])
                act(uc.ap[:, :, 2:2 + T], pu.ap[:, :N].rearrange("p (b t) -> p b t", b=nb), AF.Identity, [pu], [uc])
                cp("dve", fh.ap[:, ct], uc.ap[:, :, T:T + 2], [uc], [fh])
                ts("dve", a_.ap, uc.ap[:, :, 2:2 + T], fconvw.ap[:, ct, 2:3], None, ALU.mult, None, [uc, fconvw], [a_])
                stt("dve", a_.ap, uc.ap[:, :, 1:1 + T], fconvw.ap[:, ct, 1:2], a_.ap, ALU.mult, ALU.add, [uc, fconvw, a_], [a_])
                stt("dve", a_.ap, uc.ap[:, :, 0:T], fconvw.ap[:, ct, 0:1], a_.ap, ALU.mult, ALU.add, [uc, fconvw, a_], [a_])
                act(a_.ap, a_.ap, AF.Gelu_apprx_tanh, [a_], [a_])
                tt("dve", ffT.ap[:, ct, :], pg.ap[:, :N], a_.ap.rearrange("p b t -> p (b t)"), ALU.mult, [pg, a_], [ffT.k(ct)])
        if last_p or not prompt:
            nrow = 2 if prompt else 32
            fo = aalloc([DFF], F32)
            for q6 in range(6):
                pb = psn()
                nct = min(4, 22 - q6 * 4)
                for j in range(nct):
                    ct = q6 * 4 + j
                    tr(pb.ap[0:nrow, j * 128:(j + 1) * 128], fh.ap[:, ct].rearrange("p b c -> p (b c)"), ident_f.ap, [fh, ident_f], [pb])
                cp("act" if q6 % 2 == 0 else "dve", fo.ap[0:nrow, q6 * 512:q6 * 512 + nct * 128], pb.ap[0:nrow, 0:nct * 128], [pb], [fo])
            dma_out(fb_p if prompt else fb_s, fo.ap[0:nrow, :], fo)
        mm_stage(w_down, DFF, [(c * 128, 128) for c in range(8)], ffT, N, add_resid(N), wname="w_down")
        arelease(mt_)

        stage(9)
        pb = psn()
        for kt in range(8):
            sqv = sq[kt % 2]
            act(sqv.ap[:, :N], xT.ap[:, kt, :N], AF.Square, [xT.k(kt)], [sqv])
            mm(pb.ap[:, :N], ones_b.ap, sqv.ap[:, :N], kt == 0, kt == 7, [ones_b, sqv], [pb])
        act(rs.ap[:, :N], pb.ap[:, :N], AF.Ln, [pb], [rs], scale=1.0 / D, bias=EPS)
        act(rs.ap[:, :N], rs.ap[:, :N], AF.Exp, [rs], [rs], scale=-0.5)
        for kt in range(8):
            stt("dve", xT.ap[:, kt, :N], xT.ap[:, kt, :N], gains.ap[:, 3, kt:kt + 1], rs.ap[:, :N], ALU.mult, ALU.mult, [xT.k(kt), rs, gains], [xT.k(kt)])
        if prompt:
            for s_ in range(4):
                st = xst[s_ % 2]
                for half in range(2):
                    pb = psn()
                    for j in range(4):
                        kt = half * 4 + j
                        tr(pb.ap[:, j * 128:(j + 1) * 128], xT.ap[:, kt, s_ * 128:(s_ + 1) * 128], ident_f.ap, [xT.k(kt), ident_f], [pb])
                    cp("act" if half == 0 else "dve", st.ap[:, half * 512:(half + 1) * 512], pb.ap, [pb], [st])
                dma_out(y_p[ti * NT + s_ * 128: ti * NT + (s_ + 1) * 128, :], st.ap, st)
        else:
            st = xst[0]
            for half in range(2):
                pb = psn()
                for j in range(4):
                    kt = half * 4 + j
                    tr(pb.ap[0:NS, j * 128:(j + 1) * 128], xT.ap[:, kt, 0:NS], ident_f.ap, [xT.k(kt), ident_f], [pb])
                cp("act" if half == 0 else "dve", st.ap[0:NS, half * 512:(half + 1) * 512], pb.ap[0:NS, :], [pb], [st])
            dma_out(y_s, st.ap[0:NS, :], st)
        arelease(mt0_)

    for ti in range(SEQ // NT):
        run_tile("p", ti)
    run_tile("s", 0)

    S.stopped = False
    S.add("sp", lambda e: e.nop(), [], out_dma_bufs)

    S.finalize()
    engsem = {}
    for name in ("pe", "act", "dve", "pool", "sp"):
        engsem[name] = es.enter_context(nc.semaphore("sem_" + name))
    allbufs = set()
    for op in S.ops:
        if op.dma:
            allbufs.add(op.sembuf)
    for i, b in enumerate(sorted(allbufs, key=lambda b: b.name)):
        b.sem = es.enter_context(nc.semaphore(f"dsem{i}"))
    with nc.Block() as block:
        @block.sync
        def _(e):
            S.replay("sp", e, engsem)

        @block.gpsimd
        def _(e):
            S.replay("pool", e, engsem)

        @block.scalar
        def _(e):
            S.replay("act", e, engsem)

        @block.vector
        def _(e):
            S.replay("dve", e, engsem)

        @block.tensor
        def _(e):
            S.replay("pe", e, engsem)
    es.close()
    return nc


_NC_CACHE = {}


def kernel(x_prompt, x_sample, mem_prompt, cache_mem_k, cache_mem_v, state_ssm_re, state_ssm_im,
           state_conv, state_ffn_conv, norm_mix, w_in, ssm_A_re, ssm_A_im, ssm_log_dt, ssm_B_re,
           ssm_B_im, ssm_C_re, ssm_C_im, ssm_D, w_glu, b_glu, conv_w, norm_ssm_out, norm_conv_out,
           w_out, norm_xattn, norm_mem, w_q, w_k, w_v, w_xo, norm_ffn, w_up, w_gate, ffn_conv_w,
           w_down, norm_final):
    f = lambda a: np.ascontiguousarray(np.asarray(a, dtype=np.float32))
    if "nc" not in _NC_CACHE:
        _NC_CACHE["nc"] = build_program()
    nc = _NC_CACHE["nc"]
    shared = {
        "norm_mix": f(norm_mix[0]), "w_in": f(w_in[0]), "A_re": f(ssm_A_re[0]), "A_im": f(ssm_A_im[0]),
        "log_dt": f(ssm_log_dt[0]), "B_re": f(ssm_B_re[0]), "B_im": f(ssm_B_im[0]), "C_re": f(ssm_C_re[0]),
        "C_im": f(ssm_C_im[0]), "D_ssm": f(ssm_D[0]).reshape(512), "w_glu": f(w_glu[0]), "b_glu": f(b_glu[0]),
        "conv_w": f(conv_w[0]), "norm_ssm_out": f(norm_ssm_out[0]), "norm_conv_out": f(norm_conv_out[0]),
        "w_out": f(w_out[0]), "norm_xattn": f(norm_xattn[0]), "norm_mem": f(norm_mem[0]), "w_q": f(w_q[0]),
        "w_k": f(w_k[0]), "w_v": f(w_v[0]), "w_xo": f(w_xo[0]), "norm_ffn": f(norm_ffn[0]), "w_up": f(w_up[0]),
        "w_gate": f(w_gate[0]), "ffn_conv_w": f(ffn_conv_w[0]), "w_down": f(w_down[0]), "norm_final": f(norm_final),
    }
    in_maps = []
    for c in range(NCORES):
        sl = slice(c * BS, (c + 1) * BS)
        m = dict(shared)
        m["x_p"] = f(x_prompt[c])
        m["x_s"] = f(x_sample[sl]).reshape(NS, D)
        m["mem_p"] = f(mem_prompt[c])
        m["ck"] = f(cache_mem_k[0, sl]).reshape(BS, NMEM, D)
        m["cv"] = f(cache_mem_v[0, sl]).reshape(BS, NMEM, D)
        m["st_re"] = f(state_ssm_re[0, sl]).reshape(BS, 2048)
        m["st_im"] = f(state_ssm_im[0, sl]).reshape(BS, 2048)
        m["st_conv"] = f(state_conv[0, sl]).reshape(BS * 2, 512)
        m["st_ffn"] = f(state_ffn_conv[0, sl]).reshape(BS * 2, DFF)
        in_maps.append(m)
    res = run_bass_kernel_spmd(nc, in_maps, core_ids=list(range(NCORES)))
    R = res.results
    kernel.last_results = R
    cat = lambda k: np.concatenate([np.asarray(r[k], dtype=np.float32) for r in R], axis=0)
    stk = lambda k: np.stack([np.asarray(r[k], dtype=np.float32) for r in R], axis=0)
    yp = stk("y_p")
    ys = cat("y_s").reshape(NCORES * BS, TS, D)
    mk = stk("mk_o").reshape(1, NCORES, NMEM, 4, 256)
    mv = stk("mv_o").reshape(1, NCORES, NMEM, 4, 256)
    srp = stk("sre_p").reshape(1, NCORES, 32, 64)
    sip = stk("sim_p").reshape(1, NCORES, 32, 64)
    cbp = stk("cb_p").reshape(1, NCORES, 2, 512)
    fbp = stk("fb_p").reshape(1, NCORES, 2, DFF)
    srs = cat("sre_s").reshape(1, NCORES * BS, 32, 64)
    sis = cat("sim_s").reshape(1, NCORES * BS, 32, 64)
    cbs = cat("cb_s").reshape(1, NCORES * BS, 2, 512)
    fbs = cat("fb_s").reshape(1, NCORES * BS, 2, DFF)
    return (yp, ys, mk, mv, srp, sip, cbp, fbp, srs, sis, cbs, fbs)
```
